# Optimizing a Trainium2 kernel written in Bass

```python
import jax, jax.numpy as jnp
from jax import lax
import numpy as np

D_MODEL = 1024
BATCH = 1
SEQ = 16384
DEPTH = 2
DEC_BATCH = 2
DEC_SEQ = 16384
PAST_LEN = 128

GRID_W = 64
BLOCK = 128
HEAD_DIM = 64
A_HEADS = 8
A_KV_HEADS = 2
B_HEADS = 8
B_KV_HEADS = 2
WINDOW = 128
IN_COLS = (A_HEADS + 2 * A_KV_HEADS + B_HEADS + 2 * B_KV_HEADS) * HEAD_DIM
MIX_WIDTH = (A_HEADS + B_HEADS) * HEAD_DIM
C_HEADS = 16
Q_LORA = 384
KV_LORA = 256
NOPE_DIM = 64
ROPE_DIM = 32
V_DIM = 64
C_DOWN_COLS = Q_LORA + KV_LORA + ROPE_DIM
D_FF = 2816
CONV_W = 3
ROPE_THETA = 10000.0
EPS = 1e-6
N_EVEN = (DEPTH + 1) // 2
N_ODD = DEPTH // 2

kernel_name = 'hybrid_axial_window_mla_encoder'


def _rms_norm(x, g):
    xf = x.astype(jnp.float32)
    y = xf * lax.rsqrt(jnp.mean(xf * xf, axis=-1, keepdims=True) + EPS)
    return (y * g.astype(jnp.float32)).astype(x.dtype)


def _rope_angles(pos, dim):
    inv_freq = ROPE_THETA ** (-jnp.arange(0, dim, 2, dtype=jnp.float32) / dim)
    ang = pos.astype(jnp.float32)[:, None] * inv_freq[None, :]
    return jnp.cos(ang), jnp.sin(ang)


def _apply_rope(x, cos, sin):
    xf = x.astype(jnp.float32)
    x1, x2 = jnp.split(xf, 2, axis=-1)
    c = cos[:, None, :]
    sn = sin[:, None, :]
    return jnp.concatenate([x1 * c - x2 * sn, x2 * c + x1 * sn], axis=-1).astype(x.dtype)


def _axial_rope(x, cos_r, sin_r, cos_c, sin_c):
    half = x.shape[-1] // 2
    return jnp.concatenate([_apply_rope(x[..., :half], cos_r, sin_r),
                            _apply_rope(x[..., half:], cos_c, sin_c)], axis=-1)


def _dense_attention(q, k, v, scale):
    b, s, hq, dk = q.shape
    hkv = k.shape[2]
    g = hq // hkv
    dv = v.shape[-1]
    nblk = s // BLOCK
    qb = jnp.moveaxis(q.reshape(b, nblk, BLOCK, hkv, g, dk), 1, 0)

    def one_block(qi):
        sc = jnp.einsum('bqkgd,bskd->bkgqs', qi, k, preferred_element_type=jnp.float32) * scale
        p = jax.nn.softmax(sc, axis=-1)
        return jnp.einsum('bkgqs,bskd->bqkgd', p.astype(v.dtype), v)

    o = lax.map(one_block, qb)
    return jnp.moveaxis(o, 0, 1).reshape(b, s, hq, dv)


def _window_sink_attention(q, k, v, sink, slopes):
    b, s, hq, d = q.shape
    hkv = k.shape[2]
    g = hq // hkv
    nblk = s // BLOCK
    qb = q.reshape(b, nblk, BLOCK, hkv, g, d)

    def band(z):
        zp = jnp.pad(z, ((0, 0), (BLOCK, BLOCK), (0, 0), (0, 0)))
        zp = zp.reshape(b, nblk + 2, BLOCK, z.shape[2], z.shape[3])
        return jnp.concatenate([zp[:, :-2], zp[:, 1:-1], zp[:, 2:]], axis=2)

    kb = band(k)
    vb = band(v)
    sc = jnp.einsum('bnqkgd,bnskd->bnkgqs', qb, kb, preferred_element_type=jnp.float32) * (d ** -0.5)
    dist = jnp.abs(jnp.arange(3 * BLOCK)[None, :] - BLOCK - jnp.arange(BLOCK)[:, None])
    kpos = (jnp.arange(nblk)[:, None] - 1) * BLOCK + jnp.arange(3 * BLOCK)[None, :]
    valid = (dist <= WINDOW)[None] & ((kpos >= 0) & (kpos < s))[:, None, :]
    bias = -slopes.reshape(hkv, g)[:, :, None, None] * dist.astype(jnp.float32)
    sc = jnp.where(valid[None, :, None, None], sc + bias, -jnp.inf)
    sink_l = sink.astype(jnp.float32).reshape(1, 1, hkv, g, 1, 1)
    m = jnp.maximum(jnp.max(sc, axis=-1, keepdims=True), sink_l)
    e = jnp.exp(sc - m)
    p = e / (jnp.sum(e, axis=-1, keepdims=True) + jnp.exp(sink_l - m))
    o = jnp.einsum('bnkgqs,bnskd->bnqkgd', p.astype(v.dtype), vb)
    return o.reshape(b, s, hq, d)


def _even_mixer(h, w_in, q_gain, k_gain, sink, w_out, axial, slopes):
    b, s, _ = h.shape
    sizes = [A_HEADS * HEAD_DIM, A_KV_HEADS * HEAD_DIM, A_KV_HEADS * HEAD_DIM,
             B_HEADS * HEAD_DIM, B_KV_HEADS * HEAD_DIM, B_KV_HEADS * HEAD_DIM]
    cuts = [int(c) for c in np.cumsum(sizes)[:-1]]
    qa, ka, va, qb, kb, vb = jnp.split(h @ w_in, cuts, axis=-1)
    qa = _axial_rope(_rms_norm(qa.reshape(b, s, A_HEADS, HEAD_DIM), q_gain), *axial)
    ka = _axial_rope(_rms_norm(ka.reshape(b, s, A_KV_HEADS, HEAD_DIM), k_gain), *axial)
    va = va.reshape(b, s, A_KV_HEADS, HEAD_DIM)
    oa = _dense_attention(qa, ka, va, HEAD_DIM ** -0.5)
    qb = qb.reshape(b, s, B_HEADS, HEAD_DIM)
    kb = kb.reshape(b, s, B_KV_HEADS, HEAD_DIM)
    vb = vb.reshape(b, s, B_KV_HEADS, HEAD_DIM)
    ob = _window_sink_attention(qb, kb, vb, sink, slopes)
    o = jnp.concatenate([oa.reshape(b, s, A_HEADS * HEAD_DIM), ob.reshape(b, s, B_HEADS * HEAD_DIM)], axis=-1)
    return o @ w_out


def _mla(h, w_down, q_gain, kv_gain, w_uq, w_ukv, w_out, cos, sin):
    b, s, _ = h.shape
    cq, ckv, k_rope = jnp.split(h @ w_down, [Q_LORA, Q_LORA + KV_LORA], axis=-1)
    q = (_rms_norm(cq, q_gain) @ w_uq).reshape(b, s, C_HEADS, NOPE_DIM + ROPE_DIM)
    q_nope, q_rope = jnp.split(q, [NOPE_DIM], axis=-1)
    kv = (_rms_norm(ckv, kv_gain) @ w_ukv).reshape(b, s, C_HEADS, NOPE_DIM + V_DIM)
    k_nope, v = jnp.split(kv, [NOPE_DIM], axis=-1)
    k_rope = _apply_rope(k_rope[:, :, None, :], cos, sin)
    q = jnp.concatenate([q_nope, _apply_rope(q_rope, cos, sin)], axis=-1)
    k = jnp.concatenate([k_nope, jnp.broadcast_to(k_rope, (b, s, C_HEADS, ROPE_DIM))], axis=-1)
    o = _dense_attention(q, k, v, (NOPE_DIM + ROPE_DIM) ** -0.5)
    return o.reshape(b, s, C_HEADS * V_DIM) @ w_out


def _conv_glu(h, w_up, conv_w, conv_b, w_down):
    s = h.shape[1]
    gate, val = jnp.split(h @ w_up, 2, axis=-1)
    pad = CONV_W // 2
    gp = jnp.pad(gate, ((0, 0), (pad, pad), (0, 0)))
    gc = conv_b
    for j in range(CONV_W):
        gc = gc + gp[:, j:j + s] * conv_w[j]
    return (jax.nn.silu(gc) * val) @ w_down


def _trunk(x, norm_mix, norm_ffn, norm_final, e_w_in, e_q_gain, e_k_gain, e_sink, e_w_out,
           o_w_down, o_q_gain, o_kv_gain, o_w_uq, o_w_ukv, o_w_out,
           f_w_up, f_conv_w, f_conv_b, f_w_down):
    s = x.shape[1]
    rows = s // GRID_W
    row = jnp.repeat(jnp.arange(rows), GRID_W)
    col = jnp.tile(jnp.arange(GRID_W), rows)
    cos_r, sin_r = _rope_angles(row, HEAD_DIM // 2)
    cos_c, sin_c = _rope_angles(col, HEAD_DIM // 2)
    axial = (cos_r, sin_r, cos_c, sin_c)
    cos_t, sin_t = _rope_angles(jnp.arange(s), ROPE_DIM)
    slopes = 2.0 ** (-8.0 * jnp.arange(1, B_HEADS + 1, dtype=jnp.float32) / B_HEADS)
    for l in range(DEPTH):
        h = _rms_norm(x, norm_mix[l])
        if l % 2 == 0:
            i = l // 2
            x = x + _even_mixer(h, e_w_in[i], e_q_gain[i], e_k_gain[i], e_sink[i], e_w_out[i], axial, slopes)
        else:
            i = l // 2
            x = x + _mla(h, o_w_down[i], o_q_gain[i], o_kv_gain[i], o_w_uq[i], o_w_ukv[i], o_w_out[i], cos_t, sin_t)
        h = _rms_norm(x, norm_ffn[l])
        x = x + _conv_glu(h, f_w_up[l], f_conv_w[l], f_conv_b[l], f_w_down[l])
    return _rms_norm(x, norm_final)


def setup_inputs(seed: int = 0) -> dict:
    key = jax.random.key(seed)
    ks = jax.random.split(key, 20)
    f32 = jnp.float32

    def w(k, shape, fan_in):
        return jax.random.normal(k, shape, f32) * (fan_in ** -0.5)

    def gain(k, shape):
        return 1.0 + 0.02 * jax.random.normal(k, shape, f32)

    return {
        'x_prompt': jax.random.normal(ks[0], (BATCH, SEQ, D_MODEL), f32),
        'x_sample': jax.random.normal(ks[1], (DEC_BATCH, DEC_SEQ, D_MODEL), f32),
        'norm_mix': gain(ks[2], (DEPTH, D_MODEL)),
        'norm_ffn': gain(ks[3], (DEPTH, D_MODEL)),
        'norm_final': gain(ks[4], (D_MODEL,)),
        'e_w_in': w(ks[5], (N_EVEN, D_MODEL, IN_COLS), D_MODEL),
        'e_q_gain': gain(ks[6], (N_EVEN, HEAD_DIM)),
        'e_k_gain': gain(ks[7], (N_EVEN, HEAD_DIM)),
        'e_sink': 0.5 * jax.random.normal(ks[8], (N_EVEN, B_HEADS), f32),
        'e_w_out': w(ks[9], (N_EVEN, MIX_WIDTH, D_MODEL), MIX_WIDTH),
        'o_w_down': w(ks[10], (N_ODD, D_MODEL, C_DOWN_COLS), D_MODEL),
        'o_q_gain': gain(ks[11], (N_ODD, Q_LORA)),
        'o_kv_gain': gain(ks[12], (N_ODD, KV_LORA)),
        'o_w_uq': w(ks[13], (N_ODD, Q_LORA, C_HEADS * (NOPE_DIM + ROPE_DIM)), Q_LORA),
        'o_w_ukv': w(ks[14], (N_ODD, KV_LORA, C_HEADS * (NOPE_DIM + V_DIM)), KV_LORA),
        'o_w_out': w(ks[15], (N_ODD, C_HEADS * V_DIM, D_MODEL), C_HEADS * V_DIM),
        'f_w_up': w(ks[16], (DEPTH, D_MODEL, 2 * D_FF), D_MODEL),
        'f_conv_w': w(ks[17], (DEPTH, CONV_W, D_FF), CONV_W),
        'f_conv_b': 0.01 * jax.random.normal(ks[18], (DEPTH, D_FF), f32),
        'f_w_down': w(ks[19], (DEPTH, D_FF, D_MODEL), D_FF),
    }


def reference(x_prompt, x_sample, norm_mix, norm_ffn, norm_final, e_w_in, e_q_gain, e_k_gain, e_sink,
              e_w_out, o_w_down, o_q_gain, o_kv_gain, o_w_uq, o_w_ukv, o_w_out,
              f_w_up, f_conv_w, f_conv_b, f_w_down):
    y_prompt = _trunk(x_prompt, norm_mix, norm_ffn, norm_final, e_w_in, e_q_gain, e_k_gain, e_sink, e_w_out,
                      o_w_down, o_q_gain, o_kv_gain, o_w_uq, o_w_ukv, o_w_out,
                      f_w_up, f_conv_w, f_conv_b, f_w_down)
    y_sample = _trunk(x_sample, norm_mix, norm_ffn, norm_final, e_w_in, e_q_gain, e_k_gain, e_sink, e_w_out,
                      o_w_down, o_q_gain, o_kv_gain, o_w_uq, o_w_ukv, o_w_out,
                      f_w_up, f_conv_w, f_conv_b, f_w_down)
    return (y_prompt, y_sample)
```

```python
import numpy as np
import ml_dtypes
from contextlib import ExitStack

import concourse.bass as bass
import concourse.mybir as mybir
from concourse.bass_utils import run_bass_kernel_spmd

F32 = mybir.dt.float32
BF16 = mybir.dt.bfloat16
AF = mybir.ActivationFunctionType
ALU = mybir.AluOpType

NCORES = 8
NCX = 1
D = 1024
NSEQ = 1
DFF = 2816
NFC = DFF // 128
EPS = 1e-6
NEG = -30000.0
ENGS = ("pe", "act", "dve", "pool", "sp")
import os as _os
STRICT = bool(int(_os.environ.get("KSTRICT", "0")))


class T:
    def __init__(self, h, name=""):
        self.h = h
        self.name = name
        self.w = {}
        self.r = {}

    def __getitem__(self, idx):
        return self.h[idx]


def _merge(dst, src):
    for k, v in src.items():
        if dst.get(k, 0) < v:
            dst[k] = v


class Ctx:
    def __init__(self, nc):
        self.nc = nc
        self.stack = ExitStack()
        self.sems = {}
        self.cnt = {}
        self.dram = {}
        for i in range(int(_os.environ.get("KDUMMY", "0"))):
            self.stack.enter_context(nc.semaphore(f"dummy{i}"))

    def sem(self, name):
        if name not in self.sems:
            self.sems[name] = self.stack.enter_context(self.nc.semaphore(name))
            self.cnt[name] = 0
        return name

    def dram_t(self, name, shape, dtype, kind=None):
        if kind is None:
            h = self.nc.dram_tensor(name, list(shape), dtype)
        else:
            h = self.nc.dram_tensor(name, list(shape), dtype, kind=kind)
        t = T(h.ap(), name)
        self.dram[name] = t
        return t


class Phase:
    def __init__(self, cx, name):
        self.cx = cx
        self.nc = cx.nc
        self.name = name
        self.ops = {e: [] for e in ENGS}
        self.waited = {e: {} for e in ENGS}
        self.esem = {e: cx.sem(f"E_{e}") for e in ENGS}
        self.stack = ExitStack()
        self.dma_sems = set()

    def sb(self, name, shape, dtype):
        h = self.stack.enter_context(self.nc.sbuf_tensor(f"{self.name}_{name}", list(shape), dtype))
        nbytes = int(np.prod(shape[1:])) * (4 if dtype == F32 else 2)
        pad = (-nbytes) % 128
        if pad:
            self.stack.enter_context(self.nc.sbuf_tensor(f"{self.name}_{name}_pad", [128, pad // 2], BF16))
        return T(h, name)

    def ps(self, name, shape, dtype=F32):
        h = self.stack.enter_context(self.nc.psum_tensor(f"{self.name}_{name}", list(shape), dtype))
        return T(h, name)

    def _emit(self, eng, waits, fn, sem, amt):
        ws = []
        wd = self.waited[eng]
        for k, v in waits.items():
            if wd.get(k, 0) >= v:
                continue
            wd[k] = v
            ws.append((k, v))
        self.ops[eng].append((ws, fn, sem, amt))

    def op(self, eng, fn, reads=(), writes=(), parts=()):
        me = self.esem[eng]
        if STRICT:
            me = None
        waits = {}
        for t in reads:
            _merge(waits, t.w)
        for t in writes:
            _merge(waits, {k: v for k, v in t.w.items() if k != me})
            _merge(waits, {k: v for k, v in t.r.items() if k != me})
        for t in parts:
            _merge(waits, {k: v for k, v in t.w.items() if k != me})
            _merge(waits, {k: v for k, v in t.r.items() if k != me})
        me = self.esem[eng]
        self.cx.cnt[me] += 1
        ev = {me: self.cx.cnt[me]}
        self._emit(eng, waits, fn, me, 1)
        for t in reads:
            _merge(t.r, ev)
        for t in writes:
            t.w = dict(ev)
            t.r = {}
        for t in parts:
            _merge(t.w, ev)
        return ev

    def dma(self, eng, sem, out_ap, in_ap, reads=(), writes=(), parts=()):
        sem = self.cx.sem(sem)
        self.dma_sems.add(sem)
        waits = {}
        for t in reads:
            _merge(waits, t.w)
        for t in writes:
            _merge(waits, t.w)
            _merge(waits, t.r)
        for t in parts:
            _merge(waits, t.r)
            _merge(waits, {k: v for k, v in t.w.items() if not k.startswith(("st_", "ld_", "cc_", "dbg_"))})
        self.cx.cnt[sem] += 16
        ev = {sem: self.cx.cnt[sem]}

        def fn(h, out_ap=out_ap, in_ap=in_ap):
            return h.dma_start(out=out_ap, in_=in_ap)

        self._emit(eng, waits, fn, sem, 16)
        for t in reads:
            _merge(t.r, ev)
        for t in writes:
            t.w = dict(ev)
            t.r = {}
        for t in parts:
            _merge(t.w, ev)
        return ev

    def fence(self):
        allw = {self.esem[e]: self.cx.cnt[self.esem[e]] for e in ENGS}
        for s in self.dma_sems:
            allw[s] = self.cx.cnt[s]
        allw = {k: v for k, v in allw.items() if v > 0}
        for e in ENGS:
            self._emit(e, dict(allw), None, None, 0)

    def finish(self):
        allw = {s: self.cx.cnt[s] for s in self.dma_sems if self.cx.cnt[s] > 0}
        for e in ENGS:
            if e != "sp":
                allw[self.esem[e]] = self.cx.cnt[self.esem[e]]
        allw = {k: v for k, v in allw.items() if v > 0}
        self._emit("sp", allw, None, None, 0)
        for t in self.cx.dram.values():
            t.w = {}
            t.r = {}
        sems = self.cx.sems
        ops = self.ops

        def run(h, lst):
            for ws, fn, sem, amt in lst:
                for k, v in ws:
                    h.wait_ge(sems[k], v)
                if fn is not None:
                    fn(h).then_inc(sems[sem], amt)

        with self.nc.Block() as block:
            @block.tensor
            def _(h):
                run(h, ops["pe"])

            @block.scalar
            def _(h):
                run(h, ops["act"])

            @block.vector
            def _(h):
                run(h, ops["dve"])

            @block.gpsimd
            def _(h):
                run(h, ops["pool"])

            @block.sync
            def _(h):
                run(h, ops["sp"])
        self.stack.close()


def MM(out, lhsT, rhs, start=True, stop=True):
    return lambda h: h.matmul(out, lhsT, rhs, start=start, stop=stop)


def TR(out, in_, ident):
    return lambda h: h.transpose(out, in_, ident)


def ACT(out, in_, func, scale=1.0, bias=0.0, accum_out=None):
    if accum_out is None:
        return lambda h: h.activation(out=out, in_=in_, func=func, bias=bias, scale=scale)
    return lambda h: h.activation(out=out, in_=in_, func=func, bias=bias, scale=scale, accum_out=accum_out)


def TS(out, in0, s1, op0, s2=None, op1=None):
    if op1 is None:
        return lambda h: h.tensor_scalar(out=out, in0=in0, scalar1=s1, scalar2=None, op0=op0)
    return lambda h: h.tensor_scalar(out=out, in0=in0, scalar1=s1, scalar2=s2, op0=op0, op1=op1)


def STT(out, in0, scalar, in1, op0, op1):
    return lambda h: h.scalar_tensor_tensor(out=out, in0=in0, scalar=scalar, in1=in1, op0=op0, op1=op1)


def TT(out, in0, in1, op):
    return lambda h: h.tensor_tensor(out=out, in0=in0, in1=in1, op=op)


def CP(out, in_):
    return lambda h: h.tensor_copy(out=out, in_=in_)


def RECIP(out, in_):
    return lambda h: h.reciprocal(out=out, in_=in_)


def MEMSET(out, val):
    return lambda h: h.memset(out, val)


class Builder:
    def __init__(self, S, debug=False):
        self.S = S
        self.CH = S // NCX
        self.NTOK = NSEQ * self.CH
        self.TL = 512
        self.NT = self.CH // self.TL
        self.NB = S // 128
        self.CB = self.CH // 128
        self.debug = debug
        import os
        self.skip = os.environ.get("KSKIP", "")
        assert self.CH % self.TL == 0

    def declare(self):
        nc = self.nc
        cx = self.cx
        CH, NTOK, S = self.CH, self.NTOK, self.S
        I = lambda n, s, d=F32: cx.dram_t(n, s, d, kind="ExternalInput")
        self.x_own = I("x_own", [NTOK, D])
        self.x_halo = I("x_halo", [NSEQ * 256, D])
        self.w_in = I("w_in", [D, 1536])
        self.w_out0 = I("w_out0", [D, D])
        self.w_dn1 = I("w_dn1", [D, 768])
        self.w_uq = I("w_uq", [384, 1536])
        self.w_ukv = I("w_ukv", [256, 2048])
        self.w_out1 = I("w_out1", [D, D])
        self.w_up = [I(f"w_up{l}", [D, 2 * DFF]) for l in range(2)]
        self.w_down = [I(f"w_down{l}", [DFF, D]) for l in range(2)]
        self.cols = I("cols", [128, 64])
        self.convw = I("convw", [128, 2 * 4 * NFC])
        self.gfin = I("gfin", [D])
        self.cb_id = I("c_ident", [128, 128], BF16)
        self.cb_rt = I("c_rt", [128, 128], BF16)
        self.cb_bd = I("c_bd", [128, 128], BF16)
        self.cb_ones = I("c_ones", [128, 128], BF16)
        self.ropeA = I("ropeA", [128, 2 * CH])
        self.ropeM = I("ropeM", [128, 2 * CH])
        self.wbias = I("wbias", [128, 8 * 3 * 128])
        self.y = cx.dram_t("y", [NTOK, D], F32, kind="ExternalOutput")
        Sd = lambda n, s, d=BF16: cx.dram_t(n, s, d)
        self.qaT = Sd("qaT", [512, NTOK])
        self.qbT = Sd("qbT", [512, NTOK])
        self.kaT_loc = Sd("kaT_loc", [NSEQ * 128, CH])
        self.kaT_all = Sd("kaT_all", [NCX * NSEQ * 128, CH])
        self.va_loc = Sd("va_loc", [NTOK, 128])
        self.va_all = Sd("va_all", [NCX * NTOK, 128])
        self.kbT = Sd("kbT", [128, NSEQ * (CH + 256)])
        self.vb = Sd("vb", [NSEQ * (CH + 256), 128])
        self.oT = Sd("oT", [1024, NTOK])
        self.xmid = Sd("xmid", [NSEQ * (CH + 2), D], F32)
        self.hx_in = Sd("hx_in", [6, D], F32)
        self.hx_all = Sd("hx_all", [NCX * 6, D], F32)
        self.x1 = Sd("x1", [NTOK, D], F32)
        self.qT = Sd("qT", [16 * 96, NTOK])
        self.knT_loc = Sd("knT_loc", [NSEQ * 1024, CH])
        self.knT_all = Sd("knT_all", [NCX * NSEQ * 1024, CH])
        self.krT_loc = Sd("krT_loc", [NSEQ * 32, CH])
        self.krT_all = Sd("krT_all", [NCX * NSEQ * 32, CH])
        self.v_loc = Sd("v_loc", [NTOK, 1024])
        self.v_all = Sd("v_all", [NCX * NTOK, 1024])
        if NCX == 1:
            self.kaT_all, self.va_all = self.kaT_loc, self.va_loc
            self.knT_all, self.krT_all, self.v_all = self.knT_loc, self.krT_loc, self.v_loc

    def load_consts(self, ph, names):
        out = {}
        srcs = {"ident": self.cb_id, "rt": self.cb_rt, "bd": self.cb_bd, "ones": self.cb_ones}
        for n in names:
            t = ph.sb("c_" + n, [128, 128], BF16)
            ph.dma("sp", "ld_c_" + n, t[:], srcs[n][:, :], reads=[srcs[n]], writes=[t])
            out[n] = t
        cols = ph.sb("cols", [128, 64], F32)
        ph.dma("sp", "ld_cols", cols[:], self.cols[:, :], reads=[self.cols], writes=[cols])
        out["cols"] = cols
        return out

    def load_w_scaled(self, ph, wt, src, nchunks, ncols, cols, gcol0, stg, col0=0, dst_col0=0):
        PIECE = stg[0].h.shape[1]
        k = 0
        for c in range(nchunks):
            for p0 in range(0, ncols, PIECE):
                pn = min(PIECE, ncols - p0)
                st = stg[k % 2]
                ph.dma("sp", f"ld_stg{k % 2}", st[:, 0:pn], src[c * 128:(c + 1) * 128, col0 + p0:col0 + p0 + pn],
                       reads=[src], writes=[st])
                ph.op("dve", TS(wt[:, c, dst_col0 + p0:dst_col0 + p0 + pn], st[:, 0:pn],
                                cols[:, gcol0 + c:gcol0 + c + 1], ALU.mult),
                      reads=[st, cols], parts=[wt])
                k += 1

    def load_w_cast(self, ph, wt, src, nchunks, name):
        step = 4
        for c0 in range(0, nchunks, step):
            c1 = min(nchunks, c0 + step)
            ph.dma("pool", "ld_w_" + name, wt[:, c0:c1, :],
                   src[c0 * 128:c1 * 128, :].rearrange("(c p) n -> p c n", p=128),
                   reads=[src], parts=[wt])

    def norm_T(self, ph, B, src_rows, nb, col_off=0, ident=None, preloaded=0):
        xt, xn, hT = B["xt"], B["xn"], B["hT"]
        ss, rt, rstd, junk, cols = B["ss"], B["rt"], B["rstd"], B["junk"], B["cols"]
        nrows = preloaded if preloaded else src_rows.shape[0]
        P = min(128, nrows)
        if preloaded:
            pass
        elif nrows >= 128:
            ph.dma("sp", "ld_" + xt.name, xt[:, 0:nb, :], src_rows.rearrange("(b p) d -> p b d", p=128),
                   reads=[B["src_t"]], writes=[xt])
        else:
            ph.dma("sp", "ld_" + xt.name, xt[0:P, 0, :], src_rows, reads=[B["src_t"]], writes=[xt])
        for b in range(nb):
            ph.op("act", ACT(junk[0:P, :], xt[0:P, b, :], AF.Square, accum_out=ss[0:P, b:b + 1]),
                  reads=[xt], writes=[junk], parts=[ss])
        ph.op("act", ACT(rt[0:P, 0:nb], ss[0:P, 0:nb], AF.Sqrt, scale=1.0 / D, bias=cols[0:P, 47:48]),
              reads=[ss, cols], writes=[rt])
        ph.op("dve", RECIP(rstd[0:P, 0:nb], rt[0:P, 0:nb]), reads=[rt], writes=[rstd])
        for b in range(nb):
            ph.op("dve", TS(xn[0:P, b, :], xt[0:P, b, :], rstd[0:P, b:b + 1], ALU.mult),
                  reads=[xt, rstd], parts=[xn])
        k = B.get("tr_k", 0)
        for b in range(nb):
            for half in range(2):
                pt = B["psT"][k % 2]
                k += 1
                for q in range(4):
                    dc = half * 4 + q
                    ph.op("pe", TR(pt[:, q, :], xn[:, b, dc * 128:(dc + 1) * 128], ident[:, :]),
                          reads=[xn, ident], writes=[pt] if q == 0 else (), parts=() if q == 0 else [pt])
                c0 = col_off + b * 128
                eng = "act" if (k % 2) else "dve"
                if eng == "act":
                    ph.op("act", ACT(hT[:, half * 4:half * 4 + 4, c0:c0 + P], pt[:, :, 0:P], AF.Copy),
                          reads=[pt], parts=[hT])
                else:
                    ph.op("dve", CP(hT[:, half * 4:half * 4 + 4, c0:c0 + P], pt[:, :, 0:P]),
                          reads=[pt], parts=[hT])
        B["tr_k"] = k

    def norm_bufs(self, ph, nbmax, ncols, cols):
        B = {}
        B["xt"] = ph.sb("xt", [128, nbmax, D], F32)
        B["xn"] = ph.sb("xn", [128, nbmax, D], BF16)
        B["hT"] = ph.sb("hT", [128, 8, ncols], BF16)
        B["ss"] = ph.sb("ss", [128, 8], F32)
        B["rt"] = ph.sb("rt", [128, 8], F32)
        B["rstd"] = ph.sb("rstd", [128, 8], F32)
        B["junk"] = ph.sb("junk", [128, D], BF16)
        B["cols"] = cols
        B["psT"] = [ph.ps(f"psT{i}", [128, 4, 128], BF16) for i in range(2)]
        return B

    def phase_A(self, pname="A"):
        ph = Phase(self.cx, pname)
        CH, TL, NT = self.CH, self.TL, self.NT
        C = self.load_consts(ph, ["ident", "rt", "bd"])
        cols = C["cols"]
        w = ph.sb("w", [128, 8, 1536], BF16)
        stg = [ph.sb(f"stg{i}", [128, 1536], F32) for i in range(2)]
        self.load_w_scaled(ph, w, self.w_in, 8, 1536, cols, 0, stg)
        ropes = [ph.sb(f"rope{i}", [128, 2, TL], F32) for i in range(2)]
        ropeA3 = self.ropeA.h.rearrange("p (a t) -> p a t", a=2)
        zt = ph.sb("zt", [2, D], F32)
        ph.op("dve", MEMSET(zt[:, :], 0.0), writes=[zt])
        for s_ in range(NSEQ):
            for rr in (s_ * (CH + 2), s_ * (CH + 2) + CH + 1):
                ph.dma("pool", "st_zt", self.xmid[rr:rr + 1, :], zt[0:1, :], reads=[zt], parts=[self.xmid])
        B = self.norm_bufs(ph, 4, TL, cols)
        psP = [ph.ps(f"psP{i}", [128, 512]) for i in range(3)]
        psX = [ph.ps(f"psX{i}", [128, 512]) for i in range(2)]
        sq = ph.sb("sq", [128, 512], BF16)
        rtq = ph.sb("rtq", [128, 512], F32)
        rinv = ph.sb("rinv", [128, 512], F32)
        qn = ph.sb("qn", [128, 512], BF16)
        t1 = ph.sb("t1", [128, 512], F32)
        t2 = ph.sb("t2", [128, 512], F32)
        ob = [ph.sb(f"ob{i}", [128, 512], BF16) for i in range(3)]
        vo = [ph.sb(f"vo{i}", [128, 256], BF16) for i in range(2)]
        kp = [0]
        ko = [0]
        kv = [0]

        def proj_chunk(ntok, wc0):
            pp = psP[kp[0] % 3]
            kp[0] += 1
            for dc in range(8):
                ph.op("pe", MM(pp[:, 0:ntok], w[:, dc, wc0:wc0 + 128], B["hT"][:, dc, 0:ntok], dc == 0, dc == 7),
                      reads=[w, B["hT"]], writes=[pp] if dc == 0 else (), parts=() if dc == 0 else [pp])
            return pp

        def store(o, n, dst_t, dst_ap):
            ph.dma("pool", "st_" + o.name, dst_ap, o[:, 0:n], reads=[o], parts=[dst_t])

        def plain(pp, ntok):
            o = ob[ko[0] % 3]
            ko[0] += 1
            ph.op("act", ACT(o[:, 0:ntok], pp[:, 0:ntok], AF.Copy), reads=[pp], writes=[o])
            return o

        def normrope(pp, ntok, gcol, rope):
            c0 = 0
            CH = 0
            o = ob[ko[0] % 3]
            ko[0] += 1
            n = ntok
            ph.op("act", ACT(sq[:, 0:n], pp[:, 0:n], AF.Square), reads=[pp], writes=[sq])
            px = psX[0]
            ph.op("pe", MM(px[:, 0:n], C["bd"][:], sq[:, 0:n]), reads=[C["bd"], sq], writes=[px])
            ph.op("act", ACT(rtq[:, 0:n], px[:, 0:n], AF.Sqrt, scale=1.0 / 64, bias=cols[:, 47:48]),
                  reads=[px, cols], writes=[rtq])
            ph.op("dve", RECIP(rinv[:, 0:n], rtq[:, 0:n]), reads=[rtq], writes=[rinv])
            ph.op("dve", STT(qn[:, 0:n], pp[:, 0:n], cols[:, gcol:gcol + 1], rinv[:, 0:n], ALU.mult, ALU.mult),
                  reads=[pp, cols, rinv], writes=[qn])
            px2 = psX[1]
            ph.op("pe", MM(px2[:, 0:n], C["rt"][:], qn[:, 0:n]), reads=[C["rt"], qn], writes=[px2])
            ph.op("dve", TT(t1[:, 0:n], qn[:, 0:n], rope[:, 0, 0:n], ALU.mult), reads=[qn, rope], writes=[t1])
            ph.op("dve", TT(t2[:, 0:n], px2[:, 0:n], rope[:, 1, 0:n], ALU.mult),
                  reads=[px2, rope], writes=[t2])
            ph.op("dve", TT(o[:, 0:n], t1[:, 0:n], t2[:, 0:n], ALU.add), reads=[t1, t2], writes=[o])
            return o

        units = [("own", s, j) for s in range(NSEQ) for j in range(NT)] + [("halo", s, 0) for s in range(NSEQ)]
        for kind, s, j in units:
            if kind == "own":
                tok0 = s * CH + j * TL
                ntok, nb = TL, TL // 128
                B["src_t"] = self.x_own
                self.norm_T(ph, B, self.x_own[tok0:tok0 + TL, :], nb, ident=C["ident"])
                rope = ropes[j % 2]
                ph.dma("sp", "ld_" + rope.name, rope[:, :, :], ropeA3[:, :, j * TL:(j + 1) * TL], reads=[self.ropeA],
                       writes=[rope])
                for i in range(4):
                    o = normrope(proj_chunk(ntok, i * 128), ntok, 32, rope)
                    store(o, ntok, self.qaT, self.qaT[i * 128:(i + 1) * 128, tok0:tok0 + ntok])
                o = normrope(proj_chunk(ntok, 512), ntok, 33, rope)
                store(o, ntok, self.kaT_loc, self.kaT_loc[s * 128:(s + 1) * 128, j * TL:j * TL + ntok])
                for i in range(4):
                    o = plain(proj_chunk(ntok, 640 + i * 128), ntok)
                    store(o, ntok, self.qbT, self.qbT[i * 128:(i + 1) * 128, tok0:tok0 + ntok])
                o = plain(proj_chunk(ntok, 1152), ntok)
                kb0 = s * (CH + 256) + 128 + j * TL
                store(o, ntok, self.kbT, self.kbT[:, kb0:kb0 + ntok])
                for b in range(nb):
                    pp = psP[kp[0] % 3]
                    kp[0] += 1
                    for dc in range(8):
                        ph.op("pe", MM(pp[:, 0:256], B["hT"][:, dc, b * 128:(b + 1) * 128], w[:, dc, 1280:1536],
                                       dc == 0, dc == 7),
                              reads=[w, B["hT"]], writes=[pp] if dc == 0 else (), parts=() if dc == 0 else [pp])
                    v = vo[kv[0] % 2]
                    kv[0] += 1
                    ph.op("act", ACT(v[:, 0:256], pp[:, 0:256], AF.Copy), reads=[pp], writes=[v])
                    ph.dma("pool", "st_" + v.name, self.va_loc[tok0 + b * 128:tok0 + (b + 1) * 128, :], v[:, 0:128],
                           reads=[v], parts=[self.va_loc])
                    ph.dma("pool", "st_" + v.name, self.vb[kb0 + b * 128:kb0 + (b + 1) * 128, :], v[:, 128:256],
                           reads=[v], parts=[self.vb])
            else:
                ntok, nb = 256, 2
                B["src_t"] = self.x_halo
                self.norm_T(ph, B, self.x_halo[s * 256:(s + 1) * 256, :], nb, ident=C["ident"])
                o = plain(proj_chunk(ntok, 1152), ntok)
                base = s * (CH + 256)
                ph.dma("pool", "st_" + o.name, self.kbT[:, base:base + 128], o[:, 0:128], reads=[o], parts=[self.kbT])
                ph.dma("pool", "st_" + o.name, self.kbT[:, base + 128 + CH:base + 256 + CH], o[:, 128:256],
                       reads=[o], parts=[self.kbT])
                for b in range(nb):
                    pp = psP[kp[0] % 3]
                    kp[0] += 1
                    for dc in range(8):
                        ph.op("pe", MM(pp[:, 0:128], B["hT"][:, dc, b * 128:(b + 1) * 128], w[:, dc, 1408:1536],
                                       dc == 0, dc == 7),
                              reads=[w, B["hT"]], writes=[pp] if dc == 0 else (), parts=() if dc == 0 else [pp])
                    v = vo[kv[0] % 2]
                    kv[0] += 1
                    ph.op("act", ACT(v[:, 0:128], pp[:, 0:128], AF.Copy), reads=[pp], writes=[v])
                    r0 = base + (0 if b == 0 else 128 + CH)
                    ph.dma("pool", "st_" + v.name, self.vb[r0:r0 + 128, :], v[:, 0:128], reads=[v], parts=[self.vb])
        ph.finish()

    def phase_AG(self, name, pairs):
        ph = Phase(self.cx, name)
        for i, (src, dst) in enumerate(pairs):
            sem = self.cx.sem(f"cc_{name}_{i}")
            ph.dma_sems.add(sem)
            waits = {}
            _merge(waits, src.w)
            _merge(waits, dst.w)
            _merge(waits, dst.r)
            self.cx.cnt[sem] += 1
            ev = {sem: self.cx.cnt[sem]}

            def fn(h, src=src, dst=dst):
                return h.collective_compute("AllGather", ALU.bypass, replica_groups=[list(range(NCORES))],
                                            ins=[src.h.opt()], outs=[dst.h.opt()])

            ph._emit("pool", waits, fn, sem, 1)
            ph._emit("pool", dict(ev), None, None, 0)
            dst.w = dict(ev)
            dst.r = {}
            _merge(src.r, ev)
        ph.finish()

    def attn_bufs(self, ph, qkinds):
        A = {}
        A["S"] = [ph.ps(f"S{i}", [128, 512]) for i in range(3)]
        A["O"] = [ph.ps(f"O{i}", [128, 512]) for i in range(2)]
        A["P"] = [ph.sb(f"P{i}", [128, 512], BF16) for i in range(4)]
        A["q"] = {}
        for kind in qkinds:
            A["q"][kind] = [ph.sb(f"q{kind}{i}", [128, 512], BF16) for i in range(2)]
            for t in A["q"][kind]:
                ph.op("dve", MEMSET(t[:, :], 0.0), writes=[t])
        A["rc"] = ph.sb("rc", [128, 512], F32)
        A["rc2"] = ph.sb("rc2", [128, 512], F32)
        A["osb"] = [ph.sb(f"osb{i}", [128, 512], BF16) for i in range(2)]
        A["ku"] = 0
        A["ks"] = 0
        A["kp"] = 0
        return A

    def attn_finish(self, ph, A, O, num_lo, dst_t, dst_ap, extra_den=None):
        nr = slice(0, 64) if num_lo else slice(64, 128)
        dr = slice(64, 128) if num_lo else slice(0, 64)
        rc = A["rc"]
        osb = A["osb"][A["ku"] % 2]
        rc2 = A["rc2"]
        if extra_den is None:
            ph.op("dve", CP(rc2[dr, :], O[dr, :]), reads=[O], writes=[rc2])
        else:
            et, eap = extra_den
            ph.op("dve", TS(rc2[dr, :], O[dr, :], eap(dr), ALU.add), reads=[O, et], writes=[rc2])
        ph.op("dve", RECIP(rc2[dr, :], rc2[dr, :]), reads=[rc2], writes=[rc2])
        kmode = int(_os.environ.get("KMODE", "9"))
        if kmode < 5:
            return
        ph.dma("pool", "mv_rc", rc[nr, :], rc2[dr, :], reads=[rc2], writes=[rc])
        if kmode < 6:
            return
        ph.op("dve", TT(osb[nr, :], O[nr, :], rc[nr, :], ALU.mult), reads=[O, rc], writes=[osb])
        if kmode < 7:
            return
        ph.dma("pool", "st_" + osb.name, dst_ap, osb[nr, :], reads=[osb], parts=[dst_t])

    def attn_dense_unit(self, ph, A, qkind, q_src_t, q_src_ap, qrows, Kt, K_fn, Vt, V_fn, nkb, scale, num_lo, dst_t, dst_ap,
                        after_qload=None):
        import os
        u = A["ku"]
        qs = A["q"][qkind][u % 2]
        O = A["O"][u % 2]
        ph.dma("sp", "ld_" + qs.name, qs[qrows, :], q_src_ap, reads=[q_src_t], writes=[qs])
        if after_qload is not None:
            after_qload()
        S, P = A["S"], A["P"]
        s0, p0 = A["ks"], A["kp"]

        NS = int(os.environ.get("KNS", "3"))

        def mm1(i):
            Sb = S[(s0 + i) % NS]
            ph.op("pe", MM(Sb[:, :], K_fn(i), qs[:, :]), reads=[Kt, qs], writes=[Sb])

        def ex(i):
            Sb = S[(s0 + i) % NS]
            Pb = P[(p0 + i) % 4]
            ph.op("act", ACT(Pb[:, :], Sb[:, :], AF.Copy if os.environ.get("KCOPY") else AF.Exp, scale=scale), reads=[Sb], writes=[Pb])

        def mm2(i):
            Pb = P[(p0 + i) % 4]
            ph.op("pe", MM(O[:, :], V_fn(i), Pb[:, :], i == 0, i == nkb - 1), reads=[Vt, Pb],
                  writes=[O] if i == 0 else (), parts=() if i == 0 else [O])

        import os
        kmode = int(os.environ.get("KMODE", "9"))
        if kmode >= 1:
            mm1(0)
            if nkb > 1:
                mm1(1)
            for i in range(nkb):
                if kmode >= 2:
                    ex(i)
                if i + 2 < nkb:
                    mm1(i + 2)
                if kmode >= 3:
                    mm2(i)
        A["ks"] = s0 + nkb
        A["kp"] = p0 + nkb
        if kmode >= 4:
            self.attn_finish(ph, A, O, num_lo, dst_t, dst_ap)
        A["ku"] = u + 1

    def phase_C(self):
        ph = Phase(self.cx, "C")
        CH, TL, NT, NB, CB, NTOK, S = self.CH, self.TL, self.NT, self.NB, self.CB, self.NTOK, self.S
        cols = ph.sb("cols", [128, 64], F32)
        ph.dma("sp", "ld_cols", cols[:], self.cols[:, :], reads=[self.cols], writes=[cols])
        wb = ph.sb("wb", [128, 8 * 3 * 128], F32)
        ph.dma("sp", "ld_wb", wb[:], self.wbias[:, :], reads=[self.wbias], writes=[wb])
        esink = ph.sb("esink", [128, 8], F32)
        ph.op("act", ACT(esink[:, :], cols[:, 39:47], AF.Exp), reads=[cols], writes=[esink])
        Kt = ph.sb("Kt", [128, S], BF16)
        Vt = ph.sb("Vt", [128, NB, 192], BF16)
        KBt = ph.sb("KBt", [128, CH + 256], BF16)
        VBt = ph.sb("VBt", [128, CB + 2, 192], BF16)
        tmp = [ph.sb(f"tmp{i}", [128, 384], F32) for i in range(2)]
        A = self.attn_bufs(ph, ["g0", "g1"])
        ph.op("dve", MEMSET(Vt[:, :, 64:128], 1.0), writes=[Vt])
        ph.op("dve", MEMSET(VBt[:, :, 64:128], 1.0), writes=[VBt])
        import os
        for s in range(NSEQ):
            for r in range(NCX):
                ph.dma("sp", "ld_Kt", Kt[:, r * CH:(r + 1) * CH],
                       self.kaT_all[(r * NSEQ + s) * 128:(r * NSEQ + s + 1) * 128, :],
                       reads=[self.kaT_all], parts=[Kt])
                src = self.va_all[r * NTOK + s * CH:r * NTOK + (s + 1) * CH, :].rearrange("(b p) c -> p b c", p=128)
                for g in range(2):
                    ph.dma("sp", "ld_Vt", Vt[:, r * CB:(r + 1) * CB, g * 128:g * 128 + 64], src[:, :, g * 64:(g + 1) * 64],
                           reads=[self.va_all], parts=[Vt])
            for h in range(int(os.environ.get("KHEADS", "8")) if "d" not in self.skip else 0):
                g = h // 4
                qrows = slice(g * 64, (g + 1) * 64)
                for j in range(NT):
                    tok0 = s * CH + j * TL
                    self.attn_dense_unit(
                        ph, A, f"g{g}", self.qaT, self.qaT[h * 64:(h + 1) * 64, tok0:tok0 + TL], qrows,
                        Kt, lambda i: Kt[:, i * 128:(i + 1) * 128],
                        Vt, lambda i, g=g: Vt[:, i, g * 64:g * 64 + 128],
                        NB, 0.125, g == 0, self.oT, self.oT[h * 64:(h + 1) * 64, tok0:tok0 + TL])
            base = s * (CH + 256)
            ph.dma("sp", "ld_KBt", KBt[:, :], self.kbT[:, base:base + CH + 256], reads=[self.kbT], writes=[KBt])
            srcv = self.vb[base:base + CH + 256, :].rearrange("(b p) c -> p b c", p=128)
            for g in range(2):
                ph.dma("sp", "ld_VBt", VBt[:, :, g * 128:g * 128 + 64], srcv[:, :, g * 64:(g + 1) * 64],
                       reads=[self.vb], parts=[VBt])
            for h in range(8 if "w" not in self.skip else 0):
                g = h // 4
                qrows = slice(g * 64, (g + 1) * 64)
                for j in range(NT):
                    tok0 = s * CH + j * TL
                    u = A["ku"]
                    qs = A["q"][f"g{g}"][u % 2]
                    O = A["O"][u % 2]
                    ph.dma("sp", "ld_" + qs.name, qs[qrows, :], self.qbT[h * 64:(h + 1) * 64, tok0:tok0 + TL],
                           reads=[self.qbT], writes=[qs])
                    nbq = TL // 128
                    for kbp in range(nbq * j, nbq * j + nbq + 2):
                        n_lo = max(nbq * j, kbp - 2)
                        n_hi = min(nbq * j + nbq - 1, kbp)
                        nq = n_hi - n_lo + 1
                        qc0 = (n_lo - nbq * j) * 128
                        ncol = nq * 128
                        r0 = n_lo - kbp + 2
                        Sb = A["S"][A["ks"] % 3]
                        A["ks"] += 1
                        Pb = A["P"][A["kp"] % 4]
                        A["kp"] += 1
                        tb = tmp[kbp % 2]
                        ph.op("pe", MM(Sb[:, 0:ncol], KBt[:, kbp * 128:(kbp + 1) * 128], qs[:, qc0:qc0 + ncol]),
                              reads=[KBt, qs], writes=[Sb])
                        wo = (h * 3 + r0) * 128
                        ph.op("dve", STT(tb[:, 0:ncol], Sb[:, 0:ncol], 0.125, wb[:, wo:wo + ncol], ALU.mult, ALU.add),
                              reads=[Sb, wb], writes=[tb])
                        if kbp == 0:
                            bias = cols[:, 48:49]
                        elif kbp == CB + 1:
                            bias = cols[:, 49:50]
                        else:
                            bias = 0.0
                        ph.op("act", ACT(Pb[:, 0:ncol], tb[:, 0:ncol], AF.Exp, bias=bias), reads=[tb, cols], writes=[Pb])
                        first = kbp == nbq * j
                        last = kbp == nbq * j + nbq + 1
                        ph.op("pe", MM(O[:, qc0:qc0 + ncol], VBt[:, kbp, g * 64:g * 64 + 128], Pb[:, 0:ncol], first, last),
                              reads=[VBt, Pb], writes=[O] if first else (), parts=() if first else [O])
                    self.attn_finish(ph, A, O, g == 0, self.oT, self.oT[512 + h * 64:512 + (h + 1) * 64, tok0:tok0 + TL],
                                     extra_den=(esink, lambda dr, h=h: esink[dr, h:h + 1]))
                    A["ku"] = u + 1
        ph.finish()

    def phase_O(self, name, w_src, x_src):
        ph = Phase(self.cx, name)
        CH, TL, NT = self.CH, self.TL, self.NT
        w = ph.sb("w", [128, 8, D], BF16)
        self.load_w_cast(ph, w, w_src, 8, name)
        oTt = [ph.sb(f"oT{i}", [128, 8, TL], BF16) for i in range(2)]
        xt = [ph.sb(f"xt{i}", [128, TL // 128, D], F32) for i in range(2)]
        pso = [ph.ps(f"pso{i}", [128, 512]) for i in range(4)]
        k = 0
        for s in range(NSEQ):
            for j in range(NT):
                tok0 = s * CH + j * TL
                u = s * NT + j
                ot, x = oTt[u % 2], xt[u % 2]
                ph.dma("sp", "ld_" + ot.name, ot[:, :, :],
                       self.oT[:, tok0:tok0 + TL].rearrange("(c p) t -> p c t", p=128), reads=[self.oT], writes=[ot])
                ph.dma("sp", "ld_" + x.name, x[:, :, :],
                       x_src[tok0:tok0 + TL, :].rearrange("(b p) d -> p b d", p=128), reads=[x_src], writes=[x])
                for b in range(TL // 128):
                    for half in range(2):
                        pp = pso[k % 4]
                        k += 1
                        for c in range(8):
                            ph.op("pe", MM(pp[:, :], ot[:, c, b * 128:(b + 1) * 128], w[:, c, half * 512:(half + 1) * 512],
                                           c == 0, c == 7),
                                  reads=[ot, w], writes=[pp] if c == 0 else (), parts=() if c == 0 else [pp])
                        ph.op("dve", TT(x[:, b, half * 512:(half + 1) * 512], pp[:, :], x[:, b, half * 512:(half + 1) * 512],
                                        ALU.add), reads=[pp, x], parts=[x])
                r0 = s * (CH + 2) + 1 + j * TL
                ph.dma("pool", "st_" + x.name, self.xmid[r0:r0 + TL, :].rearrange("(b p) d -> p b d", p=128), x[:, :, :],
                       reads=[x], parts=[self.xmid])
        ph.finish()

    def phase_X(self, name):
        ph = Phase(self.cx, name)
        CH = self.CH
        cols = ph.sb("cols", [128, 64], F32)
        ph.dma("sp", "ld_cols", cols[:], self.cols[:, :], reads=[self.cols], writes=[cols])
        for s in range(NSEQ):
            for side in range(2):
                k = s * 2 + side
                row = s * (CH + 2) + (1 if side == 0 else CH)
                ph.dma("pool", "st_hx", self.hx_in[k:k + 1, :], self.xmid[row:row + 1, :], reads=[self.xmid],
                       parts=[self.hx_in])
        sem = self.cx.sem(f"cc_{name}")
        ph.dma_sems.add(sem)
        waits = {}
        _merge(waits, self.hx_in.w)
        _merge(waits, self.hx_all.w)
        _merge(waits, self.hx_all.r)
        self.cx.cnt[sem] += 1
        ev = {sem: self.cx.cnt[sem]}
        src, dst = self.hx_in, self.hx_all

        def fn(h):
            return h.collective_compute("AllGather", ALU.bypass, replica_groups=[list(range(NCORES))],
                                        ins=[src.h.opt()], outs=[dst.h.opt()])

        ph._emit("pool", waits, fn, sem, 1)
        ph._emit("pool", dict(ev), None, None, 0)
        dst.w = dict(ev)
        dst.r = {}
        cand = ph.sb("cand", [8, NCORES, D], F32)
        acc = ph.sb("acc", [8, D], F32)
        g3 = self.hx_all.h.rearrange("(r k) d -> k r d", k=6)
        for s in range(NSEQ):
            for side in range(2):
                k = s * 2 + side
                ks = s * 2 + (1 - side)
                ph.dma("sp", "ld_cand", cand[k:k + 1, :, :], g3[ks:ks + 1, :, :], reads=[self.hx_all], parts=[cand])
        ph.op("dve", TS(acc[0:6, :], cand[0:6, 0, :], cols[0:6, 50:51], ALU.mult), reads=[cand, cols], writes=[acc])
        for r in range(1, NCORES):
            ph.op("dve", STT(acc[0:6, :], cand[0:6, r, :], cols[0:6, 50 + r:51 + r], acc[0:6, :], ALU.mult, ALU.add),
                  reads=[cand, cols, acc], writes=[acc])
        for s in range(NSEQ):
            for side in range(2):
                k = s * 2 + side
                row = s * (CH + 2) + (0 if side == 0 else CH + 1)
                ph.dma("pool", "st_acc", self.xmid[row:row + 1, :], acc[k:k + 1, :], reads=[acc], parts=[self.xmid])
        ph.finish()

    def phase_F(self, name, l, final):
        ph = Phase(self.cx, name)
        CH = self.CH
        TLF = 256
        nbF = TLF // 128
        C = self.load_consts(ph, ["ident"])
        cols = C["cols"]
        cw = ph.sb("cw", [128, 2 * 4 * NFC], F32)
        ph.dma("sp", "ld_cw", cw[:], self.convw[:, :], reads=[self.convw], writes=[cw])
        wup = ph.sb("wup", [128, 8, 2 * DFF], BF16)
        wdn = ph.sb("wdn", [128, NFC, D], BF16)
        stg = [ph.sb(f"stg{i}", [128, DFF], F32) for i in range(2)]
        self.load_w_scaled(ph, wup, self.w_up[l], 8, 2 * DFF, cols, 8 if l == 0 else 24, stg)
        self.load_w_cast(ph, wdn, self.w_down[l], NFC, name)
        B = self.norm_bufs(ph, nbF, TLF + 2, cols)
        B2 = {k: B[k] for k in ("hT", "ss", "rt", "rstd", "junk", "cols", "psT")}
        B2["xt"] = ph.sb("xh", [128, 1, D], F32)
        B2["xn"] = ph.sb("xhn", [128, 1, D], BF16)
        ph.op("dve", MEMSET(B2["xn"][:, 0, :], 0.0), writes=[B2["xn"]])
        uT = ph.sb("uT", [128, NFC, TLF], BF16)
        gc = ph.sb("gc", [128, TLF], F32)
        sg = ph.sb("sg", [128, TLF], F32)
        if final:
            gf = ph.sb("gf", [128, D], F32)
            ph.dma("sp", "ld_gf", gf[:], self.gfin.h.partition_broadcast(128), reads=[self.gfin], writes=[gf])
            ss2 = ph.sb("ss2", [128, 4], F32)
            rt2 = ph.sb("rt2", [128, 4], F32)
            rs2 = ph.sb("rs2", [128, 4], F32)
        psG = [ph.ps(f"psG{i}", [128, 512]) for i in range(2)]
        psV = ph.ps("psV", [128, 512])
        psH = ph.ps("psH", [128, 512])
        psD = [ph.ps(f"psD{i}", [128, 512]) for i in range(2)]
        hT = B["hT"]
        xt = B["xt"]
        kd = 0
        cwc = lambda q, fc: cw[:, (l * 4 + q) * NFC + fc:(l * 4 + q) * NFC + fc + 1]
        for s in range(NSEQ):
            for j in range(CH // TLF):
                t0 = j * TLF
                r0 = s * (CH + 2) + 1 + t0
                B["src_t"] = self.xmid
                self.norm_T(ph, B, self.xmid[r0:r0 + TLF, :], nbF, ident=C["ident"])
                xh = B2["xt"]
                ph.dma("sp", "ld_xh", xh[0:1, 0, :], self.xmid[r0 - 1:r0, :], reads=[self.xmid], writes=[xh])
                ph.dma("sp", "ld_xh", xh[1:2, 0, :], self.xmid[r0 + TLF:r0 + TLF + 1, :], reads=[self.xmid], parts=[xh])
                B2["tr_k"] = B.get("tr_k", 0)
                self.norm_T(ph, B2, None, 1, col_off=TLF, ident=C["ident"], preloaded=2)
                B["tr_k"] = B2["tr_k"]
                for fc in range(NFC):
                    pg = psG[fc % 2]
                    for dc in range(8):
                        ph.op("pe", MM(pg[:, 0:TLF], wup[:, dc, fc * 128:(fc + 1) * 128], hT[:, dc, 0:TLF], dc == 0, dc == 7),
                              reads=[wup, hT], writes=[pg] if dc == 0 else (), parts=() if dc == 0 else [pg])
                    for dc in range(8):
                        ph.op("pe", MM(psH[:, 0:2], wup[:, dc, fc * 128:(fc + 1) * 128], hT[:, dc, TLF:TLF + 2], dc == 0, dc == 7),
                              reads=[wup, hT], writes=[psH] if dc == 0 else (), parts=() if dc == 0 else [psH])
                    for dc in range(8):
                        ph.op("pe", MM(psV[:, 0:TLF], wup[:, dc, DFF + fc * 128:DFF + (fc + 1) * 128], hT[:, dc, 0:TLF],
                                       dc == 0, dc == 7),
                              reads=[wup, hT], writes=[psV] if dc == 0 else (), parts=() if dc == 0 else [psV])
                    ph.op("act", ACT(gc[:, 0:TLF], pg[:, 0:TLF], AF.Identity, scale=cwc(1, fc), bias=cwc(3, fc)),
                          reads=[pg, cw], writes=[gc])
                    ph.op("dve", STT(gc[:, 1:TLF], pg[:, 0:TLF - 1], cwc(0, fc), gc[:, 1:TLF], ALU.mult, ALU.add),
                          reads=[pg, cw, gc], parts=[gc])
                    ph.op("dve", STT(gc[:, 0:TLF - 1], pg[:, 1:TLF], cwc(2, fc), gc[:, 0:TLF - 1], ALU.mult, ALU.add),
                          reads=[pg, cw, gc], parts=[gc])
                    ph.op("dve", STT(gc[:, 0:1], psH[:, 0:1], cwc(0, fc), gc[:, 0:1], ALU.mult, ALU.add),
                          reads=[psH, cw, gc], parts=[gc])
                    ph.op("dve", STT(gc[:, TLF - 1:TLF], psH[:, 1:2], cwc(2, fc), gc[:, TLF - 1:TLF], ALU.mult, ALU.add),
                          reads=[psH, cw, gc], parts=[gc])
                    ph.op("act", ACT(sg[:, 0:TLF], gc[:, 0:TLF], AF.Silu), reads=[gc], writes=[sg])
                    ph.op("dve", TT(uT[:, fc, :], sg[:, 0:TLF], psV[:, 0:TLF], ALU.mult), reads=[sg, psV], parts=[uT])
                for b in range(nbF):
                    for half in range(2):
                        pd = psD[kd % 2]
                        kd += 1
                        for fc in range(NFC):
                            ph.op("pe", MM(pd[:, :], uT[:, fc, b * 128:(b + 1) * 128], wdn[:, fc, half * 512:(half + 1) * 512],
                                           fc == 0, fc == NFC - 1),
                                  reads=[uT, wdn], writes=[pd] if fc == 0 else (), parts=() if fc == 0 else [pd])
                        ph.op("dve", TT(xt[:, b, half * 512:(half + 1) * 512], pd[:, :], xt[:, b, half * 512:(half + 1) * 512],
                                        ALU.add), reads=[pd, xt], parts=[xt])
                tok0 = s * CH + t0
                if not final:
                    ph.dma("pool", "st_xt", self.x1[tok0:tok0 + TLF, :].rearrange("(b p) d -> p b d", p=128),
                           xt[:, 0:nbF, :], reads=[xt], parts=[self.x1])
                else:
                    junk = B["junk"]
                    for b in range(nbF):
                        ph.op("act", ACT(junk[:, :], xt[:, b, :], AF.Square, accum_out=ss2[:, b:b + 1]),
                              reads=[xt], writes=[junk], parts=[ss2])
                    ph.op("act", ACT(rt2[:, 0:nbF], ss2[:, 0:nbF], AF.Sqrt, scale=1.0 / D, bias=cols[:, 47:48]),
                          reads=[ss2, cols], writes=[rt2])
                    ph.op("dve", RECIP(rs2[:, 0:nbF], rt2[:, 0:nbF]), reads=[rt2], writes=[rs2])
                    for b in range(nbF):
                        ph.op("dve", STT(xt[:, b, :], xt[:, b, :], rs2[:, b:b + 1], gf[:, :], ALU.mult, ALU.mult),
                              reads=[xt, rs2, gf], parts=[xt])
                    ph.dma("pool", "st_xt", self.y[tok0:tok0 + TLF, :].rearrange("(b p) d -> p b d", p=128),
                           xt[:, 0:nbF, :], reads=[xt], parts=[self.y])
        ph.finish()

    def phase_P(self):
        ph = Phase(self.cx, "P")
        CH, TL, NT = self.CH, self.TL, self.NT
        C = self.load_consts(ph, ["ident", "rt", "ones"])
        cols = C["cols"]
        wd = ph.sb("wd", [128, 8, 768], BF16)
        wuq = ph.sb("wuq", [128, 3, 1536], BF16)
        wukv = ph.sb("wukv", [128, 2, 2048], BF16)
        stg = [ph.sb(f"stg{i}", [128, 2048], F32) for i in range(2)]
        self.load_w_scaled(ph, wd, self.w_dn1, 8, 768, cols, 16, stg)
        self.load_w_scaled(ph, wuq, self.w_uq, 3, 1536, cols, 34, stg)
        self.load_w_scaled(ph, wukv, self.w_ukv, 2, 2048, cols, 37, stg)
        ropes = [ph.sb(f"rope{i}", [128, 2, TL], F32) for i in range(2)]
        ropeM3 = self.ropeM.h.rearrange("p (a t) -> p a t", a=2)
        ropebox = [None]
        B = self.norm_bufs(ph, 4, TL, cols)
        B["src_t"] = self.x1
        hT = B["hT"]
        psP = [ph.ps(f"psP{i}", [128, 512]) for i in range(4)]
        psX = [ph.ps(f"psX{i}", [128, 512]) for i in range(2)]
        sq = ph.sb("sq", [128, 3, 512], BF16)
        rtq = ph.sb("rtq", [128, 512], F32)
        rinv = ph.sb("rinv", [128, 512], F32)
        cqn = ph.sb("cqn", [128, 3, 512], BF16)
        ckvn = ph.sb("ckvn", [128, 2, 512], BF16)
        qr = ph.sb("qr", [128, 512], BF16)
        t1 = ph.sb("t1", [128, 512], F32)
        t2 = ph.sb("t2", [128, 512], F32)
        ob = [ph.sb(f"ob{i}", [128, 512], BF16) for i in range(3)]
        vo = [ph.sb(f"vo{i}", [128, 1024], BF16) for i in range(2)]
        kp = [0]
        ko = [0]
        kv = [0]

        def proj(M, n, lhs_fn, rhs_fn, nk):
            pp = psP[kp[0] % 4]
            kp[0] += 1
            for c in range(nk):
                lt, la = lhs_fn(c)
                rt_, ra = rhs_fn(c)
                ph.op("pe", MM(pp[0:M, 0:n], la, ra, c == 0, c == nk - 1), reads=[lt, rt_],
                      writes=[pp] if c == 0 else (), parts=() if c == 0 else [pp])
            return pp

        def lownorm(nch, wc0, dim, dst):
            pps = []
            for c in range(nch):
                pp = proj(128, TL, lambda dc, c=c: (wd, wd[:, dc, wc0 + c * 128:wc0 + (c + 1) * 128]),
                          lambda dc: (hT, hT[:, dc, 0:TL]), 8)
                ph.op("act", ACT(sq[:, c, :], pp[:, :], AF.Square), reads=[pp], parts=[sq])
                pps.append(pp)
            px = psX[0]
            for c in range(nch):
                ph.op("pe", MM(px[:, :], C["ones"][:], sq[:, c, :], c == 0, c == nch - 1), reads=[C["ones"], sq],
                      writes=[px] if c == 0 else (), parts=() if c == 0 else [px])
            ph.op("act", ACT(rtq[:, :], px[:, :], AF.Sqrt, scale=1.0 / dim, bias=cols[:, 47:48]),
                  reads=[px, cols], writes=[rtq])
            ph.op("dve", RECIP(rinv[:, :], rtq[:, :]), reads=[rtq], writes=[rinv])
            for c in range(nch):
                ph.op("dve", TT(dst[:, c, :], pps[c][:, :], rinv[:, :], ALU.mult), reads=[pps[c], rinv], parts=[dst])

        def rope_apply(src, M, c0):
            o = ob[ko[0] % 3]
            ko[0] += 1
            px2 = psX[1]
            ph.op("pe", MM(px2[0:M, :], C["rt"][0:M, 0:M], src[0:M, :]), reads=[C["rt"], src], writes=[px2])
            rope = ropebox[0]
            ph.op("dve", TT(t1[0:M, :], src[0:M, :], rope[0:M, 0, :], ALU.mult), reads=[src, rope], writes=[t1])
            ph.op("dve", TT(t2[0:M, :], px2[0:M, :], rope[0:M, 1, :], ALU.mult),
                  reads=[px2, rope], writes=[t2])
            ph.op("dve", TT(o[0:M, :], t1[0:M, :], t2[0:M, :], ALU.add), reads=[t1, t2], writes=[o])
            return o

        def plain(pp, M):
            o = ob[ko[0] % 3]
            ko[0] += 1
            ph.op("act", ACT(o[0:M, :], pp[0:M, :], AF.Copy), reads=[pp], writes=[o])
            return o

        for s in range(NSEQ):
            for j in range(NT):
                tok0 = s * CH + j * TL
                c0 = j * TL
                self.norm_T(ph, B, self.x1[tok0:tok0 + TL, :], TL // 128, ident=C["ident"])
                ropebox[0] = ropes[j % 2]
                ph.dma("sp", "ld_" + ropebox[0].name, ropebox[0][:, :, :], ropeM3[:, :, j * TL:(j + 1) * TL],
                       reads=[self.ropeM], writes=[ropebox[0]])
                lownorm(3, 0, 384, cqn)
                lownorm(2, 384, 256, ckvn)
                pp = proj(128, TL, lambda dc: (wd, wd[:, dc, 640:768]), lambda dc: (hT, hT[:, dc, 0:TL]), 8)
                ph.op("act", ACT(qr[:, :], pp[:, :], AF.Copy), reads=[pp], writes=[qr])
                o = rope_apply(qr, 128, c0)
                ph.dma("pool", "st_" + o.name, self.krT_loc[s * 32:(s + 1) * 32, c0:c0 + TL], o[96:128, :],
                       reads=[o], parts=[self.krT_loc])
                for i in range(8):
                    pp = proj(128, TL, lambda c, i=i: (wuq, wuq[:, c, i * 128:(i + 1) * 128]),
                              lambda c: (cqn, cqn[:, c, :]), 3)
                    o = plain(pp, 128)
                    for hh in range(2):
                        h = 2 * i + hh
                        ph.dma("pool", "st_" + o.name, self.qT[h * 96:h * 96 + 64, tok0:tok0 + TL], o[hh * 64:(hh + 1) * 64, :],
                               reads=[o], parts=[self.qT])
                for i in range(4):
                    pp = proj(128, TL, lambda c, i=i: (wuq, wuq[:, c, 1024 + i * 128:1024 + (i + 1) * 128]),
                              lambda c: (cqn, cqn[:, c, :]), 3)
                    ph.op("act", ACT(qr[:, :], pp[:, :], AF.Copy), reads=[pp], writes=[qr])
                    o = rope_apply(qr, 128, c0)
                    for hh in range(4):
                        h = 4 * i + hh
                        ph.dma("pool", "st_" + o.name, self.qT[h * 96 + 64:h * 96 + 96, tok0:tok0 + TL],
                               o[hh * 32:(hh + 1) * 32, :], reads=[o], parts=[self.qT])
                for i in range(8):
                    pp = proj(128, TL, lambda c, i=i: (wukv, wukv[:, c, i * 128:(i + 1) * 128]),
                              lambda c: (ckvn, ckvn[:, c, :]), 2)
                    o = plain(pp, 128)
                    ph.dma("pool", "st_" + o.name, self.knT_loc[s * 1024 + i * 128:s * 1024 + (i + 1) * 128, c0:c0 + TL],
                           o[:, :], reads=[o], parts=[self.knT_loc])
                for b in range(TL // 128):
                    v = vo[kv[0] % 2]
                    kv[0] += 1
                    for half in range(2):
                        pp = proj(128, 512, lambda c, b=b: (ckvn, ckvn[:, c, b * 128:(b + 1) * 128]),
                                  lambda c, half=half: (wukv, wukv[:, c, 1024 + half * 512:1024 + (half + 1) * 512]), 2)
                        ph.op("act", ACT(v[:, half * 512:(half + 1) * 512], pp[:, :], AF.Copy), reads=[pp],
                              writes=[v] if half == 0 else (), parts=() if half == 0 else [v])
                    ph.dma("pool", "st_" + v.name, self.v_loc[tok0 + b * 128:tok0 + (b + 1) * 128, :], v[:, :],
                           reads=[v], parts=[self.v_loc])
        ph.finish()

    def phase_H(self):
        ph = Phase(self.cx, "H")
        CH, TL, NT, NB, CB, NTOK, S = self.CH, self.TL, self.NT, self.NB, self.CB, self.NTOK, self.S
        Ks = [ph.sb(f"Ks{i}", [128, S], BF16) for i in range(2)]
        Vs = [ph.sb(f"Vs{i}", [128, NB, 192], BF16) for i in range(2)]
        A = self.attn_bufs(ph, ["m"])
        for i in range(2):
            ph.op("dve", MEMSET(Vs[i][:, :, 64:128], 1.0), writes=[Vs[i]])
            ph.op("dve", MEMSET(Ks[i][64:128, :], 0.0), writes=[Ks[i]])
        heads = [(s, h) for s in range(NSEQ) for h in range(16)]

        def load_head(idx):
            if idx >= len(heads):
                return
            s, h = heads[idx]
            K = Ks[idx % 2]
            for r in range(NCX):
                base = (r * NSEQ + s) * 1024 + h * 64
                ph.dma("sp", "ld_" + K.name, K[0:64, r * CH:(r + 1) * CH], self.knT_all[base:base + 64, :],
                       reads=[self.knT_all], parts=[K])
                rb = (r * NSEQ + s) * 32
                ph.dma("sp", "ld_" + K.name, K[64:96, r * CH:(r + 1) * CH], self.krT_all[rb:rb + 32, :],
                       reads=[self.krT_all], parts=[K])
            if h % 2 == 0:
                V = Vs[(idx // 2) % 2]
                for r in range(NCX):
                    src = self.v_all[r * NTOK + s * CH:r * NTOK + (s + 1) * CH, :].rearrange("(b p) c -> p b c", p=128)
                    for hh in range(2):
                        ph.dma("sp", "ld_" + V.name, V[:, r * CB:(r + 1) * CB, hh * 128:hh * 128 + 64],
                               src[:, :, (h + hh) * 64:(h + hh + 1) * 64], reads=[self.v_all], parts=[V])

        load_head(0)
        scale = 96.0 ** -0.5
        for idx, (s, h) in enumerate(heads):
            K = Ks[idx % 2]
            V = Vs[(idx // 2) % 2]
            odd = h % 2
            for j in range(NT):
                tok0 = s * CH + j * TL
                self.attn_dense_unit(
                    ph, A, "m", self.qT, self.qT[h * 96:(h + 1) * 96, tok0:tok0 + TL], slice(0, 96),
                    K, lambda i, K=K: K[:, i * 128:(i + 1) * 128],
                    V, lambda i, V=V, odd=odd: V[:, i, odd * 64:odd * 64 + 128],
                    NB, scale, odd == 0, self.oT, self.oT[h * 64:(h + 1) * 64, tok0:tok0 + TL],
                    after_qload=(lambda idx=idx: load_head(idx + 1)) if j == 0 else None)
        ph.finish()

    def build(self):
        nc = bass.Bass("TRN2", target_bir_lowering=False)
        self.nc = nc
        self.cx = Ctx(nc)
        self.declare()
        with self.cx.stack:
            stages = self.stages if hasattr(self, "stages") else "ABCOXFPGHQYZ"
            if "A" in stages:
                self.phase_A()
            if "B" in stages and NCX > 1:
                self.phase_AG("G0", [(self.kaT_loc, self.kaT_all), (self.va_loc, self.va_all)])
            if "a" in stages:
                self.phase_A("A2")
            if "C" in stages:
                self.phase_C()
            if "O" in stages:
                self.phase_O("O0", self.w_out0, self.x_own)
            if "X" in stages and NCX > 1:
                self.phase_X("X0")
            if "F" in stages:
                self.phase_F("F0", 0, False)
            if "P" in stages:
                self.phase_P()
            if "G" in stages and NCX > 1:
                self.phase_AG("G1", [(self.knT_loc, self.knT_all), (self.krT_loc, self.krT_all),
                                     (self.v_loc, self.v_all)])
            if "H" in stages:
                self.phase_H()
            if "Q" in stages:
                self.phase_O("O1", self.w_out1, self.x1)
            if "Y" in stages and NCX > 1:
                self.phase_X("X1")
            if "Z" in stages:
                self.phase_F("F1", 1, True)
            if self.debug:
                self.debug_dump()
        return nc

    def debug_dump(self):
        ph = Phase(self.cx, "DBG")
        for name in self.debug:
            t = self.cx.dram[name]
            shape = list(t.h.shape)
            o = self.cx.dram_t("dbg_" + name, shape, t.h.dtype, kind="ExternalOutput")
            ph.dma("sp", "dbg_" + name, o.h, t.h, reads=[t], writes=[o])
        ph.finish()


def _rope_tables(CH, core):
    theta = 10000.0
    inv = (theta ** (-np.arange(0, 32, 2, dtype=np.float32) / np.float32(32))).astype(np.float32)
    t = np.arange(core * CH, (core + 1) * CH)
    row = (t // 64).astype(np.float32)
    col = (t % 64).astype(np.float32)
    pos = t.astype(np.float32)
    ang_r = row[:, None] * inv[None, :]
    ang_c = col[:, None] * inv[None, :]
    ang_t = pos[:, None] * inv[None, :]
    A = np.zeros((128, 2 * CH), np.float32)
    M = np.zeros((128, 2 * CH), np.float32)
    for p in range(128):
        d = p % 64
        a = ang_r[:, d % 16] if d < 32 else ang_c[:, (d - 32) % 16]
        A[p, :CH] = np.cos(a)
        A[p, CH:] = np.sin(a)
        m = ang_t[:, (p % 32) % 16]
        M[p, :CH] = np.cos(m)
        M[p, CH:] = np.sin(m)
    return A, M


def _consts():
    bf = ml_dtypes.bfloat16
    ident = np.eye(128, dtype=np.float32).astype(bf)
    rt = np.zeros((128, 128), np.float32)
    for blk in range(4):
        for i in range(16):
            rt[blk * 32 + i + 16, blk * 32 + i] = -1.0
            rt[blk * 32 + i, blk * 32 + i + 16] = 1.0
    bd = np.zeros((128, 128), np.float32)
    bd[:64, :64] = 1.0
    bd[64:, 64:] = 1.0
    ones = np.ones((128, 128), np.float32)
    slopes = (2.0 ** (-8.0 * np.arange(1, 9, dtype=np.float32) / 8)).astype(np.float32)
    wb = np.zeros((128, 8, 3, 128), np.float32)
    sl = np.arange(128)
    for r in range(3):
        s_rel = (2 - r - 1) * 128 + sl
        dist = np.abs(np.arange(128)[None, :] - s_rel[:, None])
        for h in range(8):
            wb[:, h, r, :] = np.where(dist <= 128, -slopes[h] * dist.astype(np.float32), NEG)
    return ident, rt.astype(bf), bd.astype(bf), ones.astype(bf), wb.reshape(128, -1)


def _colarr(v, n):
    return np.ascontiguousarray(np.asarray(v, np.float32).reshape(n, 128).T)


_CACHE = {}


def kernel(x_prompt, x_sample, norm_mix, norm_ffn, norm_final, e_w_in, e_q_gain, e_k_gain, e_sink, e_w_out,
           o_w_down, o_q_gain, o_kv_gain, o_w_uq, o_w_ukv, o_w_out, f_w_up, f_conv_w, f_conv_b, f_w_down,
           _stages=None, _debug=None):
    f32 = np.float32
    xs = np.concatenate([np.asarray(x_prompt, f32), np.asarray(x_sample, f32)], axis=0)
    NS_ALL = xs.shape[0]
    S = xs.shape[1]
    CH = S
    key = (S, _stages, tuple(_debug) if _debug else None)
    if key not in _CACHE:
        b = Builder(S, debug=_debug)
        if _stages is not None:
            b.stages = _stages
        _CACHE[key] = b.build()
    nc = _CACHE[key]

    ident, rt, bd, ones, wb = _consts()
    w_in = np.asarray(e_w_in[0], f32)
    w_in_p = np.ascontiguousarray(np.concatenate(
        [w_in[:, 0:512], w_in[:, 512:640], w_in[:, 768:1280], w_in[:, 1280:1408], w_in[:, 640:768], w_in[:, 1408:1536]], 1))
    uq = np.asarray(o_w_uq[0], f32).reshape(384, 16, 96)
    w_uq_p = np.ascontiguousarray(np.concatenate([uq[:, :, :64].reshape(384, 1024), uq[:, :, 64:].reshape(384, 512)], 1))
    ukv = np.asarray(o_w_ukv[0], f32).reshape(256, 16, 128)
    w_ukv_p = np.ascontiguousarray(np.concatenate([ukv[:, :, :64].reshape(256, 1024), ukv[:, :, 64:].reshape(256, 1024)], 1))
    convw = np.zeros((128, 2, 4, NFC), f32)
    for l in range(2):
        for q in range(3):
            convw[:, l, q, :] = _colarr(f_conv_w[l][q], NFC)
        convw[:, l, 3, :] = _colarr(f_conv_b[l], NFC)
    convw = convw.reshape(128, -1)
    shared = {
        "w_in": w_in_p, "w_out0": np.ascontiguousarray(e_w_out[0], f32), "w_dn1": np.ascontiguousarray(np.concatenate(
            [np.asarray(o_w_down[0], f32)[:, :640], np.zeros((D, 96), f32), np.asarray(o_w_down[0], f32)[:, 640:]], 1)),
        "w_uq": w_uq_p, "w_ukv": w_ukv_p, "w_out1": np.ascontiguousarray(o_w_out[0], f32),
        "w_up0": np.ascontiguousarray(f_w_up[0], f32), "w_up1": np.ascontiguousarray(f_w_up[1], f32),
        "w_down0": np.ascontiguousarray(f_w_down[0], f32), "w_down1": np.ascontiguousarray(f_w_down[1], f32),
        "convw": convw, "gfin": np.ascontiguousarray(norm_final, f32),
        "c_ident": ident, "c_rt": rt, "c_bd": bd, "c_ones": ones, "wbias": wb,
    }
    cols = np.zeros((128, 64), f32)
    cols[:, 0:8] = _colarr(norm_mix[0], 8)
    cols[:, 8:16] = _colarr(norm_ffn[0], 8)
    cols[:, 16:24] = _colarr(norm_mix[1], 8)
    cols[:, 24:32] = _colarr(norm_ffn[1], 8)
    cols[:, 32] = np.tile(np.asarray(e_q_gain[0], f32), 2)
    cols[:, 33] = np.tile(np.asarray(e_k_gain[0], f32), 2)
    cols[:, 34:37] = _colarr(o_q_gain[0], 3)
    cols[:, 37:39] = _colarr(o_kv_gain[0], 2)
    cols[:, 39:47] = np.asarray(e_sink[0], f32)[None, :]
    cols[:, 47] = EPS
    cols[:, 48] = NEG
    cols[:, 49] = NEG
    ra, rm = _rope_tables(CH, 0)
    shared.update({"cols": cols, "ropeA": ra, "ropeM": rm, "x_halo": np.zeros((NSEQ * 256, D), f32)})
    in_maps = []
    for c in range(NCORES):
        m = dict(shared)
        m["x_own"] = np.ascontiguousarray(xs[c]) if c < NS_ALL else np.zeros((S, D), f32)
        in_maps.append(m)
    res = run_bass_kernel_spmd(nc, in_maps, core_ids=list(range(NCORES)))
    if _debug:
        return res.results
    y = np.stack([np.asarray(res.results[c]["y"], f32).reshape(S, D) for c in range(NS_ALL)], 0)
    nb = np.asarray(x_prompt).shape[0]
    return (y[:nb], y[nb:])
```

```python
import numpy as np
import ml_dtypes
from contextlib import ExitStack

import concourse.bass as bass
import concourse.mybir as mybir
from concourse.bass_utils import run_bass_kernel_spmd

F32 = mybir.dt.float32
BF16 = mybir.dt.bfloat16
AF = mybir.ActivationFunctionType
ALU = mybir.AluOpType

NCORES = 8
NCX = 1
D = 1024
NSEQ = 1
DFF = 2816
NFC = DFF // 128
EPS = 1e-6
NEG = -30000.0
ENGS = ("pe", "act", "dve", "pool", "sp")
import os as _os
STRICT = bool(int(_os.environ.get("KSTRICT", "0")))


class T:
    def __init__(self, h, name=""):
        self.h = h
        self.name = name
        self.w = {}
        self.r = {}

    def __getitem__(self, idx):
        return self.h[idx]


def _merge(dst, src):
    for k, v in src.items():
        if dst.get(k, 0) < v:
            dst[k] = v


class Ctx:
    def __init__(self, nc):
        self.nc = nc
        self.stack = ExitStack()
        self.sems = {}
        self.cnt = {}
        self.dram = {}
        for i in range(int(_os.environ.get("KDUMMY", "0"))):
            self.stack.enter_context(nc.semaphore(f"dummy{i}"))

    def sem(self, name):
        if name not in self.sems:
            self.sems[name] = self.stack.enter_context(self.nc.semaphore(name))
            self.cnt[name] = 0
        return name

    def dram_t(self, name, shape, dtype, kind=None):
        if kind is None:
            h = self.nc.dram_tensor(name, list(shape), dtype)
        else:
            h = self.nc.dram_tensor(name, list(shape), dtype, kind=kind)
        t = T(h.ap(), name)
        self.dram[name] = t
        return t


class Phase:
    def __init__(self, cx, name):
        self.cx = cx
        self.nc = cx.nc
        self.name = name
        self.ops = {e: [] for e in ENGS}
        self.waited = {e: {} for e in ENGS}
        self.esem = {e: cx.sem(f"E_{e}") for e in ENGS}
        self.stack = ExitStack()
        self.dma_sems = set()

    def sb(self, name, shape, dtype):
        h = self.stack.enter_context(self.nc.sbuf_tensor(f"{self.name}_{name}", list(shape), dtype))
        nbytes = int(np.prod(shape[1:])) * (4 if dtype == F32 else 2)
        pad = (-nbytes) % 128
        if pad:
            self.stack.enter_context(self.nc.sbuf_tensor(f"{self.name}_{name}_pad", [128, pad // 2], BF16))
        return T(h, name)

    def ps(self, name, shape, dtype=F32):
        h = self.stack.enter_context(self.nc.psum_tensor(f"{self.name}_{name}", list(shape), dtype))
        return T(h, name)

    def _emit(self, eng, waits, fn, sem, amt):
        ws = []
        wd = self.waited[eng]
        for k, v in waits.items():
            if wd.get(k, 0) >= v:
                continue
            wd[k] = v
            ws.append((k, v))
        self.ops[eng].append((ws, fn, sem, amt))

    def op(self, eng, fn, reads=(), writes=(), parts=()):
        me = self.esem[eng]
        if STRICT:
            me = None
        waits = {}
        for t in reads:
            _merge(waits, t.w)
        for t in writes:
            _merge(waits, {k: v for k, v in t.w.items() if k != me})
            _merge(waits, {k: v for k, v in t.r.items() if k != me})
        for t in parts:
            _merge(waits, {k: v for k, v in t.w.items() if k != me})
            _merge(waits, {k: v for k, v in t.r.items() if k != me})
        me = self.esem[eng]
        self.cx.cnt[me] += 1
        ev = {me: self.cx.cnt[me]}
        self._emit(eng, waits, fn, me, 1)
        for t in reads:
            _merge(t.r, ev)
        for t in writes:
            t.w = dict(ev)
            t.r = {}
        for t in parts:
            _merge(t.w, ev)
        return ev

    def dma(self, eng, sem, out_ap, in_ap, reads=(), writes=(), parts=()):
        sem = self.cx.sem(sem)
        self.dma_sems.add(sem)
        waits = {}
        for t in reads:
            _merge(waits, t.w)
        for t in writes:
            _merge(waits, t.w)
            _merge(waits, t.r)
        for t in parts:
            _merge(waits, t.r)
            _merge(waits, {k: v for k, v in t.w.items() if not k.startswith(("st_", "ld_", "cc_", "dbg_"))})
        self.cx.cnt[sem] += 16
        ev = {sem: self.cx.cnt[sem]}

        def fn(h, out_ap=out_ap, in_ap=in_ap):
            return h.dma_start(out=out_ap, in_=in_ap)

        self._emit(eng, waits, fn, sem, 16)
        for t in reads:
            _merge(t.r, ev)
        for t in writes:
            t.w = dict(ev)
            t.r = {}
        for t in parts:
            _merge(t.w, ev)
        return ev

    def fence(self):
        allw = {self.esem[e]: self.cx.cnt[self.esem[e]] for e in ENGS}
        for s in self.dma_sems:
            allw[s] = self.cx.cnt[s]
        allw = {k: v for k, v in allw.items() if v > 0}
        for e in ENGS:
            self._emit(e, dict(allw), None, None, 0)

    def finish(self):
        allw = {s: self.cx.cnt[s] for s in self.dma_sems if self.cx.cnt[s] > 0}
        for e in ENGS:
            if e != "sp":
                allw[self.esem[e]] = self.cx.cnt[self.esem[e]]
        allw = {k: v for k, v in allw.items() if v > 0}
        self._emit("sp", allw, None, None, 0)
        for t in self.cx.dram.values():
            t.w = {}
            t.r = {}
        sems = self.cx.sems
        ops = self.ops

        def run(h, lst):
            for ws, fn, sem, amt in lst:
                for k, v in ws:
                    h.wait_ge(sems[k], v)
                if fn is not None:
                    fn(h).then_inc(sems[sem], amt)

        with self.nc.Block() as block:
            @block.tensor
            def _(h):
                run(h, ops["pe"])

            @block.scalar
            def _(h):
                run(h, ops["act"])

            @block.vector
            def _(h):
                run(h, ops["dve"])

            @block.gpsimd
            def _(h):
                run(h, ops["pool"])

            @block.sync
            def _(h):
                run(h, ops["sp"])
        self.stack.close()


def MM(out, lhsT, rhs, start=True, stop=True):
    return lambda h: h.matmul(out, lhsT, rhs, start=start, stop=stop)


def TR(out, in_, ident):
    return lambda h: h.transpose(out, in_, ident)


def ACT(out, in_, func, scale=1.0, bias=0.0, accum_out=None):
    if accum_out is None:
        return lambda h: h.activation(out=out, in_=in_, func=func, bias=bias, scale=scale)
    return lambda h: h.activation(out=out, in_=in_, func=func, bias=bias, scale=scale, accum_out=accum_out)


def TS(out, in0, s1, op0, s2=None, op1=None):
    if op1 is None:
        return lambda h: h.tensor_scalar(out=out, in0=in0, scalar1=s1, scalar2=None, op0=op0)
    return lambda h: h.tensor_scalar(out=out, in0=in0, scalar1=s1, scalar2=s2, op0=op0, op1=op1)


def STT(out, in0, scalar, in1, op0, op1):
    return lambda h: h.scalar_tensor_tensor(out=out, in0=in0, scalar=scalar, in1=in1, op0=op0, op1=op1)


def TT(out, in0, in1, op):
    return lambda h: h.tensor_tensor(out=out, in0=in0, in1=in1, op=op)


def CP(out, in_):
    return lambda h: h.tensor_copy(out=out, in_=in_)


def RECIP(out, in_):
    return lambda h: h.reciprocal(out=out, in_=in_)


def MEMSET(out, val):
    return lambda h: h.memset(out, val)


class Builder:
    def __init__(self, S, debug=False):
        self.S = S
        self.CH = S // NCX
        self.NTOK = NSEQ * self.CH
        self.TL = 512
        self.NT = self.CH // self.TL
        self.NB = S // 128
        self.CB = self.CH // 128
        self.debug = debug
        import os
        self.skip = os.environ.get("KSKIP", "")
        assert self.CH % self.TL == 0

    def declare(self):
        nc = self.nc
        cx = self.cx
        CH, NTOK, S = self.CH, self.NTOK, self.S
        I = lambda n, s, d=F32: cx.dram_t(n, s, d, kind="ExternalInput")
        self.x_own = I("x_own", [NTOK, D])
        self.x_halo = I("x_halo", [NSEQ * 256, D])
        self.w_in = I("w_in", [D, 1536])
        self.w_out0 = I("w_out0", [D, D])
        self.w_dn1 = I("w_dn1", [D, 768])
        self.w_uq = I("w_uq", [384, 1536])
        self.w_ukv = I("w_ukv", [256, 2048])
        self.w_out1 = I("w_out1", [D, D])
        self.w_up = [I(f"w_up{l}", [D, 2 * DFF]) for l in range(2)]
        self.w_down = [I(f"w_down{l}", [DFF, D]) for l in range(2)]
        self.cols = I("cols", [128, 64])
        self.convw = I("convw", [128, 2 * 4 * NFC])
        self.gfin = I("gfin", [D])
        self.cb_id = I("c_ident", [128, 128], BF16)
        self.cb_rt = I("c_rt", [128, 128], BF16)
        self.cb_bd = I("c_bd", [128, 128], BF16)
        self.cb_ones = I("c_ones", [128, 128], BF16)
        self.ropeA = I("ropeA", [128, 2 * CH])
        self.ropeM = I("ropeM", [128, 2 * CH])
        self.wbias = I("wbias", [128, 8 * 3 * 128])
        self.y = cx.dram_t("y", [NTOK, D], F32, kind="ExternalOutput")
        Sd = lambda n, s, d=BF16: cx.dram_t(n, s, d)
        self.qaT = Sd("qaT", [512, NTOK])
        self.qbT = Sd("qbT", [512, NTOK])
        self.kaT_loc = Sd("kaT_loc", [NSEQ * 128, CH])
        self.kaT_all = Sd("kaT_all", [NCX * NSEQ * 128, CH])
        self.va_loc = Sd("va_loc", [NTOK, 128])
        self.va_all = Sd("va_all", [NCX * NTOK, 128])
        self.kbT = Sd("kbT", [128, NSEQ * (CH + 256)])
        self.vb = Sd("vb", [NSEQ * (CH + 256), 128])
        self.oT = Sd("oT", [1024, NTOK])
        self.xmid = Sd("xmid", [NSEQ * (CH + 2), D], F32)
        self.hx_in = Sd("hx_in", [6, D], F32)
        self.hx_all = Sd("hx_all", [NCX * 6, D], F32)
        self.x1 = Sd("x1", [NTOK, D], F32)
        self.qT = Sd("qT", [16 * 96, NTOK])
        self.knT_loc = Sd("knT_loc", [NSEQ * 1024, CH])
        self.knT_all = Sd("knT_all", [NCX * NSEQ * 1024, CH])
        self.krT_loc = Sd("krT_loc", [NSEQ * 32, CH])
        self.krT_all = Sd("krT_all", [NCX * NSEQ * 32, CH])
        self.v_loc = Sd("v_loc", [NTOK, 1024])
        self.v_all = Sd("v_all", [NCX * NTOK, 1024])
        if NCX == 1:
            self.kaT_all, self.va_all = self.kaT_loc, self.va_loc
            self.knT_all, self.krT_all, self.v_all = self.knT_loc, self.krT_loc, self.v_loc

    def load_consts(self, ph, names):
        out = {}
        srcs = {"ident": self.cb_id, "rt": self.cb_rt, "bd": self.cb_bd, "ones": self.cb_ones}
        for n in names:
            t = ph.sb("c_" + n, [128, 128], BF16)
            ph.dma("sp", "ld_c_" + n, t[:], srcs[n][:, :], reads=[srcs[n]], writes=[t])
            out[n] = t
        cols = ph.sb("cols", [128, 64], F32)
        ph.dma("sp", "ld_cols", cols[:], self.cols[:, :], reads=[self.cols], writes=[cols])
        out["cols"] = cols
        return out

    def load_w_scaled(self, ph, wt, src, nchunks, ncols, cols, gcol0, stg, col0=0, dst_col0=0):
        PIECE = stg[0].h.shape[1]
        k = 0
        for c in range(nchunks):
            for p0 in range(0, ncols, PIECE):
                pn = min(PIECE, ncols - p0)
                st = stg[k % 2]
                ph.dma("sp", f"ld_stg{k % 2}", st[:, 0:pn], src[c * 128:(c + 1) * 128, col0 + p0:col0 + p0 + pn],
                       reads=[src], writes=[st])
                ph.op("dve", TS(wt[:, c, dst_col0 + p0:dst_col0 + p0 + pn], st[:, 0:pn],
                                cols[:, gcol0 + c:gcol0 + c + 1], ALU.mult),
                      reads=[st, cols], parts=[wt])
                k += 1

    def load_w_cast(self, ph, wt, src, nchunks, name):
        step = 4
        for c0 in range(0, nchunks, step):
            c1 = min(nchunks, c0 + step)
            ph.dma("pool", "ld_w_" + name, wt[:, c0:c1, :],
                   src[c0 * 128:c1 * 128, :].rearrange("(c p) n -> p c n", p=128),
                   reads=[src], parts=[wt])

    def norm_T(self, ph, B, src_rows, nb, col_off=0, ident=None, preloaded=0):
        xt, xn, hT = B["xt"], B["xn"], B["hT"]
        ss, rt, rstd, junk, cols = B["ss"], B["rt"], B["rstd"], B["junk"], B["cols"]
        nrows = preloaded if preloaded else src_rows.shape[0]
        P = min(128, nrows)
        if preloaded:
            pass
        elif nrows >= 128:
            ph.dma("sp", "ld_" + xt.name, xt[:, 0:nb, :], src_rows.rearrange("(b p) d -> p b d", p=128),
                   reads=[B["src_t"]], writes=[xt])
        else:
            ph.dma("sp", "ld_" + xt.name, xt[0:P, 0, :], src_rows, reads=[B["src_t"]], writes=[xt])
        for b in range(nb):
            ph.op("act", ACT(junk[0:P, :], xt[0:P, b, :], AF.Square, accum_out=ss[0:P, b:b + 1]),
                  reads=[xt], writes=[junk], parts=[ss])
        ph.op("act", ACT(rt[0:P, 0:nb], ss[0:P, 0:nb], AF.Sqrt, scale=1.0 / D, bias=cols[0:P, 47:48]),
              reads=[ss, cols], writes=[rt])
        ph.op("dve", RECIP(rstd[0:P, 0:nb], rt[0:P, 0:nb]), reads=[rt], writes=[rstd])
        for b in range(nb):
            ph.op("dve", TS(xn[0:P, b, :], xt[0:P, b, :], rstd[0:P, b:b + 1], ALU.mult),
                  reads=[xt, rstd], parts=[xn])
        k = B.get("tr_k", 0)
        for b in range(nb):
            for half in range(2):
                pt = B["psT"][k % 2]
                k += 1
                for q in range(4):
                    dc = half * 4 + q
                    ph.op("pe", TR(pt[:, q, :], xn[:, b, dc * 128:(dc + 1) * 128], ident[:, :]),
                          reads=[xn, ident], writes=[pt] if q == 0 else (), parts=() if q == 0 else [pt])
                c0 = col_off + b * 128
                eng = "act" if (k % 2) else "dve"
                if eng == "act":
                    ph.op("act", ACT(hT[:, half * 4:half * 4 + 4, c0:c0 + P], pt[:, :, 0:P], AF.Copy),
                          reads=[pt], parts=[hT])
                else:
                    ph.op("dve", CP(hT[:, half * 4:half * 4 + 4, c0:c0 + P], pt[:, :, 0:P]),
                          reads=[pt], parts=[hT])
        B["tr_k"] = k

    def norm_bufs(self, ph, nbmax, ncols, cols):
        B = {}
        B["xt"] = ph.sb("xt", [128, nbmax, D], F32)
        B["xn"] = ph.sb("xn", [128, nbmax, D], BF16)
        B["hT"] = ph.sb("hT", [128, 8, ncols], BF16)
        B["ss"] = ph.sb("ss", [128, 8], F32)
        B["rt"] = ph.sb("rt", [128, 8], F32)
        B["rstd"] = ph.sb("rstd", [128, 8], F32)
        B["junk"] = ph.sb("junk", [128, D], BF16)
        B["cols"] = cols
        B["psT"] = [ph.ps(f"psT{i}", [128, 4, 128], BF16) for i in range(2)]
        return B

    def phase_A(self, pname="A"):
        ph = Phase(self.cx, pname)
        CH, TL, NT = self.CH, self.TL, self.NT
        C = self.load_consts(ph, ["ident", "rt", "bd"])
        cols = C["cols"]
        w = ph.sb("w", [128, 8, 1536], BF16)
        stg = [ph.sb(f"stg{i}", [128, 1536], F32) for i in range(2)]
        self.load_w_scaled(ph, w, self.w_in, 8, 1536, cols, 0, stg)
        ropes = [ph.sb(f"rope{i}", [128, 2, TL], F32) for i in range(2)]
        ropeA3 = self.ropeA.h.rearrange("p (a t) -> p a t", a=2)
        zt = ph.sb("zt", [2, D], F32)
        ph.op("dve", MEMSET(zt[:, :], 0.0), writes=[zt])
        for s_ in range(NSEQ):
            for rr in (s_ * (CH + 2), s_ * (CH + 2) + CH + 1):
                ph.dma("pool", "st_zt", self.xmid[rr:rr + 1, :], zt[0:1, :], reads=[zt], parts=[self.xmid])
        B = self.norm_bufs(ph, 4, TL, cols)
        psP = [ph.ps(f"psP{i}", [128, 512]) for i in range(3)]
        psX = [ph.ps(f"psX{i}", [128, 512]) for i in range(2)]
        sq = ph.sb("sq", [128, 512], BF16)
        rtq = ph.sb("rtq", [128, 512], F32)
        rinv = ph.sb("rinv", [128, 512], F32)
        qn = ph.sb("qn", [128, 512], BF16)
        t1 = ph.sb("t1", [128, 512], F32)
        t2 = ph.sb("t2", [128, 512], F32)
        ob = [ph.sb(f"ob{i}", [128, 512], BF16) for i in range(3)]
        vo = [ph.sb(f"vo{i}", [128, 256], BF16) for i in range(2)]
        kp = [0]
        ko = [0]
        kv = [0]

        def proj_chunk(ntok, wc0):
            pp = psP[kp[0] % 3]
            kp[0] += 1
            for dc in range(8):
                ph.op("pe", MM(pp[:, 0:ntok], w[:, dc, wc0:wc0 + 128], B["hT"][:, dc, 0:ntok], dc == 0, dc == 7),
                      reads=[w, B["hT"]], writes=[pp] if dc == 0 else (), parts=() if dc == 0 else [pp])
            return pp

        def store(o, n, dst_t, dst_ap):
            ph.dma("pool", "st_" + o.name, dst_ap, o[:, 0:n], reads=[o], parts=[dst_t])

        def plain(pp, ntok):
            o = ob[ko[0] % 3]
            ko[0] += 1
            ph.op("act", ACT(o[:, 0:ntok], pp[:, 0:ntok], AF.Copy), reads=[pp], writes=[o])
            return o

        def normrope(pp, ntok, gcol, rope):
            c0 = 0
            CH = 0
            o = ob[ko[0] % 3]
            ko[0] += 1
            n = ntok
            ph.op("act", ACT(sq[:, 0:n], pp[:, 0:n], AF.Square), reads=[pp], writes=[sq])
            px = psX[0]
            ph.op("pe", MM(px[:, 0:n], C["bd"][:], sq[:, 0:n]), reads=[C["bd"], sq], writes=[px])
            ph.op("act", ACT(rtq[:, 0:n], px[:, 0:n], AF.Sqrt, scale=1.0 / 64, bias=cols[:, 47:48]),
                  reads=[px, cols], writes=[rtq])
            ph.op("dve", RECIP(rinv[:, 0:n], rtq[:, 0:n]), reads=[rtq], writes=[rinv])
            ph.op("dve", STT(qn[:, 0:n], pp[:, 0:n], cols[:, gcol:gcol + 1], rinv[:, 0:n], ALU.mult, ALU.mult),
                  reads=[pp, cols, rinv], writes=[qn])
            px2 = psX[1]
            ph.op("pe", MM(px2[:, 0:n], C["rt"][:], qn[:, 0:n]), reads=[C["rt"], qn], writes=[px2])
            ph.op("dve", TT(t1[:, 0:n], qn[:, 0:n], rope[:, 0, 0:n], ALU.mult), reads=[qn, rope], writes=[t1])
            ph.op("dve", TT(t2[:, 0:n], px2[:, 0:n], rope[:, 1, 0:n], ALU.mult),
                  reads=[px2, rope], writes=[t2])
            ph.op("dve", TT(o[:, 0:n], t1[:, 0:n], t2[:, 0:n], ALU.add), reads=[t1, t2], writes=[o])
            return o

        units = [("own", s, j) for s in range(NSEQ) for j in range(NT)] + [("halo", s, 0) for s in range(NSEQ)]
        for kind, s, j in units:
            if kind == "own":
                tok0 = s * CH + j * TL
                ntok, nb = TL, TL // 128
                B["src_t"] = self.x_own
                self.norm_T(ph, B, self.x_own[tok0:tok0 + TL, :], nb, ident=C["ident"])
                rope = ropes[j % 2]
                ph.dma("sp", "ld_" + rope.name, rope[:, :, :], ropeA3[:, :, j * TL:(j + 1) * TL], reads=[self.ropeA],
                       writes=[rope])
                for i in range(4):
                    o = normrope(proj_chunk(ntok, i * 128), ntok, 32, rope)
                    store(o, ntok, self.qaT, self.qaT[i * 128:(i + 1) * 128, tok0:tok0 + ntok])
                o = normrope(proj_chunk(ntok, 512), ntok, 33, rope)
                store(o, ntok, self.kaT_loc, self.kaT_loc[s * 128:(s + 1) * 128, j * TL:j * TL + ntok])
                for i in range(4):
                    o = plain(proj_chunk(ntok, 640 + i * 128), ntok)
                    store(o, ntok, self.qbT, self.qbT[i * 128:(i + 1) * 128, tok0:tok0 + ntok])
                o = plain(proj_chunk(ntok, 1152), ntok)
                kb0 = s * (CH + 256) + 128 + j * TL
                store(o, ntok, self.kbT, self.kbT[:, kb0:kb0 + ntok])
                for b in range(nb):
                    pp = psP[kp[0] % 3]
                    kp[0] += 1
                    for dc in range(8):
                        ph.op("pe", MM(pp[:, 0:256], B["hT"][:, dc, b * 128:(b + 1) * 128], w[:, dc, 1280:1536],
                                       dc == 0, dc == 7),
                              reads=[w, B["hT"]], writes=[pp] if dc == 0 else (), parts=() if dc == 0 else [pp])
                    v = vo[kv[0] % 2]
                    kv[0] += 1
                    ph.op("act", ACT(v[:, 0:256], pp[:, 0:256], AF.Copy), reads=[pp], writes=[v])
                    ph.dma("pool", "st_" + v.name, self.va_loc[tok0 + b * 128:tok0 + (b + 1) * 128, :], v[:, 0:128],
                           reads=[v], parts=[self.va_loc])
                    ph.dma("pool", "st_" + v.name, self.vb[kb0 + b * 128:kb0 + (b + 1) * 128, :], v[:, 128:256],
                           reads=[v], parts=[self.vb])
            else:
                ntok, nb = 256, 2
                B["src_t"] = self.x_halo
                self.norm_T(ph, B, self.x_halo[s * 256:(s + 1) * 256, :], nb, ident=C["ident"])
                o = plain(proj_chunk(ntok, 1152), ntok)
                base = s * (CH + 256)
                ph.dma("pool", "st_" + o.name, self.kbT[:, base:base + 128], o[:, 0:128], reads=[o], parts=[self.kbT])
                ph.dma("pool", "st_" + o.name, self.kbT[:, base + 128 + CH:base + 256 + CH], o[:, 128:256],
                       reads=[o], parts=[self.kbT])
                for b in range(nb):
                    pp = psP[kp[0] % 3]
                    kp[0] += 1
                    for dc in range(8):
                        ph.op("pe", MM(pp[:, 0:128], B["hT"][:, dc, b * 128:(b + 1) * 128], w[:, dc, 1408:1536],
                                       dc == 0, dc == 7),
                              reads=[w, B["hT"]], writes=[pp] if dc == 0 else (), parts=() if dc == 0 else [pp])
                    v = vo[kv[0] % 2]
                    kv[0] += 1
                    ph.op("act", ACT(v[:, 0:128], pp[:, 0:128], AF.Copy), reads=[pp], writes=[v])
                    r0 = base + (0 if b == 0 else 128 + CH)
                    ph.dma("pool", "st_" + v.name, self.vb[r0:r0 + 128, :], v[:, 0:128], reads=[v], parts=[self.vb])
        ph.finish()

    def phase_AG(self, name, pairs):
        ph = Phase(self.cx, name)
        for i, (src, dst) in enumerate(pairs):
            sem = self.cx.sem(f"cc_{name}_{i}")
            ph.dma_sems.add(sem)
            waits = {}
            _merge(waits, src.w)
            _merge(waits, dst.w)
            _merge(waits, dst.r)
            self.cx.cnt[sem] += 1
            ev = {sem: self.cx.cnt[sem]}

            def fn(h, src=src, dst=dst):
                return h.collective_compute("AllGather", ALU.bypass, replica_groups=[list(range(NCORES))],
                                            ins=[src.h.opt()], outs=[dst.h.opt()])

            ph._emit("pool", waits, fn, sem, 1)
            ph._emit("pool", dict(ev), None, None, 0)
            dst.w = dict(ev)
            dst.r = {}
            _merge(src.r, ev)
        ph.finish()

    def attn_bufs(self, ph, qkinds):
        A = {}
        A["S"] = [ph.ps(f"S{i}", [128, 1024]) for i in range(2)]
        A["O"] = [ph.ps(f"O{i}", [128, 512]) for i in range(2)]
        A["P"] = [ph.sb(f"P{i}", [128, 1024], BF16) for i in range(3)]
        A["q"] = {}
        for kind in qkinds:
            A["q"][kind] = [ph.sb(f"q{kind}{i}", [128, 512], BF16) for i in range(2)]
            for t in A["q"][kind]:
                ph.op("dve", MEMSET(t[:, :], 0.0), writes=[t])
        A["rc"] = ph.sb("rc", [128, 512], F32)
        A["rc2"] = ph.sb("rc2", [128, 512], F32)
        A["osb"] = [ph.sb(f"osb{i}", [128, 512], BF16) for i in range(2)]
        A["ku"] = 0
        A["ks"] = 0
        A["kp"] = 0
        return A

    def attn_finish(self, ph, A, O, num_lo, dst_t, dst_ap, extra_den=None):
        nr = slice(0, 64) if num_lo else slice(64, 128)
        dr = slice(64, 128) if num_lo else slice(0, 64)
        rc = A["rc"]
        osb = A["osb"][A["ku"] % 2]
        rc2 = A["rc2"]
        if extra_den is None:
            ph.op("dve", CP(rc2[dr, :], O[dr, :]), reads=[O], writes=[rc2])
        else:
            et, eap = extra_den
            ph.op("dve", TS(rc2[dr, :], O[dr, :], eap(dr), ALU.add), reads=[O, et], writes=[rc2])
        ph.op("dve", RECIP(rc2[dr, :], rc2[dr, :]), reads=[rc2], writes=[rc2])
        kmode = int(_os.environ.get("KMODE", "9"))
        if kmode < 5:
            return
        ph.dma("pool", "mv_rc", rc[nr, :], rc2[dr, :], reads=[rc2], writes=[rc])
        if kmode < 6:
            return
        ph.op("dve", TT(osb[nr, :], O[nr, :], rc[nr, :], ALU.mult), reads=[O, rc], writes=[osb])
        if kmode < 7:
            return
        ph.dma("pool", "st_" + osb.name, dst_ap, osb[nr, :], reads=[osb], parts=[dst_t])

    def attn_dense_unit(self, ph, A, qkind, q_src_t, q_src_ap, qrows, Kt, K_fn, Vt, V_fn, nkb, scale, num_lo, dst_t, dst_ap,
                        after_qload=None):
        import os
        u = A["ku"]
        qs = A["q"][qkind][u % 2]
        O = A["O"][u % 2]
        ph.dma("sp", "ld_" + qs.name, qs[qrows, :], q_src_ap, reads=[q_src_t], writes=[qs])
        if after_qload is not None:
            after_qload()
        S, P = A["S"], A["P"]
        s0, p0 = A["ks"], A["kp"]

        npair = nkb // 2
        assert nkb % 2 == 0

        def mm1(i):
            Sb = S[(s0 + i // 2) % 2]
            c = (i % 2) * 512
            ph.op("pe", MM(Sb[:, c:c + 512], K_fn(i), qs[:, :]), reads=[Kt, qs],
                  writes=[Sb] if i % 2 == 0 else (), parts=() if i % 2 == 0 else [Sb])

        def ex(p):
            Sb = S[(s0 + p) % 2]
            Pb = P[(p0 + p) % 3]
            ph.op("act", ACT(Pb[:, :], Sb[:, :], AF.Exp, scale=scale), reads=[Sb], writes=[Pb])

        def mm2(i):
            Pb = P[(p0 + i // 2) % 3]
            c = (i % 2) * 512
            ph.op("pe", MM(O[:, :], V_fn(i), Pb[:, c:c + 512], i == 0, i == nkb - 1), reads=[Vt, Pb],
                  writes=[O] if i == 0 else (), parts=() if i == 0 else [O])

        for i in range(min(4, nkb)):
            mm1(i)
        for p in range(npair):
            ex(p)
            if p + 2 < npair:
                mm1(2 * p + 4)
                mm1(2 * p + 5)
            mm2(2 * p)
            mm2(2 * p + 1)
        kmode = 9
        A["ks"] = s0 + npair
        A["kp"] = p0 + npair
        if kmode >= 4:
            self.attn_finish(ph, A, O, num_lo, dst_t, dst_ap)
        A["ku"] = u + 1

    def phase_C(self):
        ph = Phase(self.cx, "C")
        CH, TL, NT, NB, CB, NTOK, S = self.CH, self.TL, self.NT, self.NB, self.CB, self.NTOK, self.S
        cols = ph.sb("cols", [128, 64], F32)
        ph.dma("sp", "ld_cols", cols[:], self.cols[:, :], reads=[self.cols], writes=[cols])
        wb = ph.sb("wb", [128, 8 * 3 * 128], F32)
        ph.dma("sp", "ld_wb", wb[:], self.wbias[:, :], reads=[self.wbias], writes=[wb])
        esink = ph.sb("esink", [128, 8], F32)
        ph.op("act", ACT(esink[:, :], cols[:, 39:47], AF.Exp), reads=[cols], writes=[esink])
        Kt = ph.sb("Kt", [128, S], BF16)
        Vt = ph.sb("Vt", [128, NB, 192], BF16)
        KBt = ph.sb("KBt", [128, CH + 256], BF16)
        VBt = ph.sb("VBt", [128, CB + 2, 192], BF16)
        tmp = [ph.sb(f"tmp{i}", [128, 384], F32) for i in range(2)]
        A = self.attn_bufs(ph, ["g0", "g1"])
        ph.op("dve", MEMSET(Vt[:, :, 64:128], 1.0), writes=[Vt])
        ph.op("dve", MEMSET(VBt[:, :, 64:128], 1.0), writes=[VBt])
        import os
        for s in range(NSEQ):
            for r in range(NCX):
                ph.dma("sp", "ld_Kt", Kt[:, r * CH:(r + 1) * CH],
                       self.kaT_all[(r * NSEQ + s) * 128:(r * NSEQ + s + 1) * 128, :],
                       reads=[self.kaT_all], parts=[Kt])
                src = self.va_all[r * NTOK + s * CH:r * NTOK + (s + 1) * CH, :].rearrange("(b p) c -> p b c", p=128)
                for g in range(2):
                    ph.dma("sp", "ld_Vt", Vt[:, r * CB:(r + 1) * CB, g * 128:g * 128 + 64], src[:, :, g * 64:(g + 1) * 64],
                           reads=[self.va_all], parts=[Vt])
            for h in range(int(os.environ.get("KHEADS", "8")) if "d" not in self.skip else 0):
                g = h // 4
                qrows = slice(g * 64, (g + 1) * 64)
                for j in range(NT):
                    tok0 = s * CH + j * TL
                    self.attn_dense_unit(
                        ph, A, f"g{g}", self.qaT, self.qaT[h * 64:(h + 1) * 64, tok0:tok0 + TL], qrows,
                        Kt, lambda i: Kt[:, i * 128:(i + 1) * 128],
                        Vt, lambda i, g=g: Vt[:, i, g * 64:g * 64 + 128],
                        NB, 0.125, g == 0, self.oT, self.oT[h * 64:(h + 1) * 64, tok0:tok0 + TL])
            base = s * (CH + 256)
            ph.dma("sp", "ld_KBt", KBt[:, :], self.kbT[:, base:base + CH + 256], reads=[self.kbT], writes=[KBt])
            srcv = self.vb[base:base + CH + 256, :].rearrange("(b p) c -> p b c", p=128)
            for g in range(2):
                ph.dma("sp", "ld_VBt", VBt[:, :, g * 128:g * 128 + 64], srcv[:, :, g * 64:(g + 1) * 64],
                       reads=[self.vb], parts=[VBt])
            for h in range(8 if "w" not in self.skip else 0):
                g = h // 4
                qrows = slice(g * 64, (g + 1) * 64)
                for j in range(NT):
                    tok0 = s * CH + j * TL
                    u = A["ku"]
                    qs = A["q"][f"g{g}"][u % 2]
                    O = A["O"][u % 2]
                    ph.dma("sp", "ld_" + qs.name, qs[qrows, :], self.qbT[h * 64:(h + 1) * 64, tok0:tok0 + TL],
                           reads=[self.qbT], writes=[qs])
                    nbq = TL // 128
                    for kbp in range(nbq * j, nbq * j + nbq + 2):
                        n_lo = max(nbq * j, kbp - 2)
                        n_hi = min(nbq * j + nbq - 1, kbp)
                        nq = n_hi - n_lo + 1
                        qc0 = (n_lo - nbq * j) * 128
                        ncol = nq * 128
                        r0 = n_lo - kbp + 2
                        Sb = A["S"][A["ks"] % 2]
                        A["ks"] += 1
                        Pb = A["P"][A["kp"] % 3]
                        A["kp"] += 1
                        tb = tmp[kbp % 2]
                        ph.op("pe", MM(Sb[:, 0:ncol], KBt[:, kbp * 128:(kbp + 1) * 128], qs[:, qc0:qc0 + ncol]),
                              reads=[KBt, qs], writes=[Sb])
                        wo = (h * 3 + r0) * 128
                        ph.op("dve", STT(tb[:, 0:ncol], Sb[:, 0:ncol], 0.125, wb[:, wo:wo + ncol], ALU.mult, ALU.add),
                              reads=[Sb, wb], writes=[tb])
                        if kbp == 0:
                            bias = cols[:, 48:49]
                        elif kbp == CB + 1:
                            bias = cols[:, 49:50]
                        else:
                            bias = 0.0
                        ph.op("act", ACT(Pb[:, 0:ncol], tb[:, 0:ncol], AF.Exp, bias=bias), reads=[tb, cols], writes=[Pb])
                        first = kbp == nbq * j
                        last = kbp == nbq * j + nbq + 1
                        ph.op("pe", MM(O[:, qc0:qc0 + ncol], VBt[:, kbp, g * 64:g * 64 + 128], Pb[:, 0:ncol], first, last),
                              reads=[VBt, Pb], writes=[O] if first else (), parts=() if first else [O])
                    self.attn_finish(ph, A, O, g == 0, self.oT, self.oT[512 + h * 64:512 + (h + 1) * 64, tok0:tok0 + TL],
                                     extra_den=(esink, lambda dr, h=h: esink[dr, h:h + 1]))
                    A["ku"] = u + 1
        ph.finish()

    def phase_O(self, name, w_src, x_src):
        ph = Phase(self.cx, name)
        CH, TL, NT = self.CH, self.TL, self.NT
        w = ph.sb("w", [128, 8, D], BF16)
        self.load_w_cast(ph, w, w_src, 8, name)
        oTt = [ph.sb(f"oT{i}", [128, 8, TL], BF16) for i in range(2)]
        xt = [ph.sb(f"xt{i}", [128, TL // 128, D], F32) for i in range(2)]
        pso = [ph.ps(f"pso{i}", [128, 512]) for i in range(4)]
        k = 0
        for s in range(NSEQ):
            for j in range(NT):
                tok0 = s * CH + j * TL
                u = s * NT + j
                ot, x = oTt[u % 2], xt[u % 2]
                ph.dma("sp", "ld_" + ot.name, ot[:, :, :],
                       self.oT[:, tok0:tok0 + TL].rearrange("(c p) t -> p c t", p=128), reads=[self.oT], writes=[ot])
                ph.dma("sp", "ld_" + x.name, x[:, :, :],
                       x_src[tok0:tok0 + TL, :].rearrange("(b p) d -> p b d", p=128), reads=[x_src], writes=[x])
                for b in range(TL // 128):
                    for half in range(2):
                        pp = pso[k % 4]
                        k += 1
                        for c in range(8):
                            ph.op("pe", MM(pp[:, :], ot[:, c, b * 128:(b + 1) * 128], w[:, c, half * 512:(half + 1) * 512],
                                           c == 0, c == 7),
                                  reads=[ot, w], writes=[pp] if c == 0 else (), parts=() if c == 0 else [pp])
                        ph.op("dve", TT(x[:, b, half * 512:(half + 1) * 512], pp[:, :], x[:, b, half * 512:(half + 1) * 512],
                                        ALU.add), reads=[pp, x], parts=[x])
                r0 = s * (CH + 2) + 1 + j * TL
                ph.dma("pool", "st_" + x.name, self.xmid[r0:r0 + TL, :].rearrange("(b p) d -> p b d", p=128), x[:, :, :],
                       reads=[x], parts=[self.xmid])
        ph.finish()

    def phase_X(self, name):
        ph = Phase(self.cx, name)
        CH = self.CH
        cols = ph.sb("cols", [128, 64], F32)
        ph.dma("sp", "ld_cols", cols[:], self.cols[:, :], reads=[self.cols], writes=[cols])
        for s in range(NSEQ):
            for side in range(2):
                k = s * 2 + side
                row = s * (CH + 2) + (1 if side == 0 else CH)
                ph.dma("pool", "st_hx", self.hx_in[k:k + 1, :], self.xmid[row:row + 1, :], reads=[self.xmid],
                       parts=[self.hx_in])
        sem = self.cx.sem(f"cc_{name}")
        ph.dma_sems.add(sem)
        waits = {}
        _merge(waits, self.hx_in.w)
        _merge(waits, self.hx_all.w)
        _merge(waits, self.hx_all.r)
        self.cx.cnt[sem] += 1
        ev = {sem: self.cx.cnt[sem]}
        src, dst = self.hx_in, self.hx_all

        def fn(h):
            return h.collective_compute("AllGather", ALU.bypass, replica_groups=[list(range(NCORES))],
                                        ins=[src.h.opt()], outs=[dst.h.opt()])

        ph._emit("pool", waits, fn, sem, 1)
        ph._emit("pool", dict(ev), None, None, 0)
        dst.w = dict(ev)
        dst.r = {}
        cand = ph.sb("cand", [8, NCORES, D], F32)
        acc = ph.sb("acc", [8, D], F32)
        g3 = self.hx_all.h.rearrange("(r k) d -> k r d", k=6)
        for s in range(NSEQ):
            for side in range(2):
                k = s * 2 + side
                ks = s * 2 + (1 - side)
                ph.dma("sp", "ld_cand", cand[k:k + 1, :, :], g3[ks:ks + 1, :, :], reads=[self.hx_all], parts=[cand])
        ph.op("dve", TS(acc[0:6, :], cand[0:6, 0, :], cols[0:6, 50:51], ALU.mult), reads=[cand, cols], writes=[acc])
        for r in range(1, NCORES):
            ph.op("dve", STT(acc[0:6, :], cand[0:6, r, :], cols[0:6, 50 + r:51 + r], acc[0:6, :], ALU.mult, ALU.add),
                  reads=[cand, cols, acc], writes=[acc])
        for s in range(NSEQ):
            for side in range(2):
                k = s * 2 + side
                row = s * (CH + 2) + (0 if side == 0 else CH + 1)
                ph.dma("pool", "st_acc", self.xmid[row:row + 1, :], acc[k:k + 1, :], reads=[acc], parts=[self.xmid])
        ph.finish()

    def phase_F(self, name, l, final):
        ph = Phase(self.cx, name)
        CH = self.CH
        TLF = 256
        nbF = TLF // 128
        C = self.load_consts(ph, ["ident"])
        cols = C["cols"]
        cw = ph.sb("cw", [128, 2 * 4 * NFC], F32)
        ph.dma("sp", "ld_cw", cw[:], self.convw[:, :], reads=[self.convw], writes=[cw])
        wup = ph.sb("wup", [128, 8, 2 * DFF], BF16)
        wdn = ph.sb("wdn", [128, NFC, D], BF16)
        stg = [ph.sb(f"stg{i}", [128, DFF], F32) for i in range(2)]
        self.load_w_scaled(ph, wup, self.w_up[l], 8, 2 * DFF, cols, 8 if l == 0 else 24, stg)
        self.load_w_cast(ph, wdn, self.w_down[l], NFC, name)
        B = self.norm_bufs(ph, nbF, TLF + 2, cols)
        B2 = {k: B[k] for k in ("hT", "ss", "rt", "rstd", "junk", "cols", "psT")}
        B2["xt"] = ph.sb("xh", [128, 1, D], F32)
        B2["xn"] = ph.sb("xhn", [128, 1, D], BF16)
        ph.op("dve", MEMSET(B2["xn"][:, 0, :], 0.0), writes=[B2["xn"]])
        uT = ph.sb("uT", [128, NFC, TLF], BF16)
        gc = ph.sb("gc", [128, TLF], F32)
        sg = ph.sb("sg", [128, TLF], F32)
        if final:
            gf = ph.sb("gf", [128, D], F32)
            ph.dma("sp", "ld_gf", gf[:], self.gfin.h.partition_broadcast(128), reads=[self.gfin], writes=[gf])
            ss2 = ph.sb("ss2", [128, 4], F32)
            rt2 = ph.sb("rt2", [128, 4], F32)
            rs2 = ph.sb("rs2", [128, 4], F32)
        psG = [ph.ps(f"psG{i}", [128, 512]) for i in range(2)]
        psV = ph.ps("psV", [128, 512])
        psH = ph.ps("psH", [128, 512])
        psD = [ph.ps(f"psD{i}", [128, 512]) for i in range(2)]
        hT = B["hT"]
        xt = B["xt"]
        kd = 0
        cwc = lambda q, fc: cw[:, (l * 4 + q) * NFC + fc:(l * 4 + q) * NFC + fc + 1]
        for s in range(NSEQ):
            for j in range(CH // TLF):
                t0 = j * TLF
                r0 = s * (CH + 2) + 1 + t0
                B["src_t"] = self.xmid
                self.norm_T(ph, B, self.xmid[r0:r0 + TLF, :], nbF, ident=C["ident"])
                xh = B2["xt"]
                ph.dma("sp", "ld_xh", xh[0:1, 0, :], self.xmid[r0 - 1:r0, :], reads=[self.xmid], writes=[xh])
                ph.dma("sp", "ld_xh", xh[1:2, 0, :], self.xmid[r0 + TLF:r0 + TLF + 1, :], reads=[self.xmid], parts=[xh])
                B2["tr_k"] = B.get("tr_k", 0)
                self.norm_T(ph, B2, None, 1, col_off=TLF, ident=C["ident"], preloaded=2)
                B["tr_k"] = B2["tr_k"]
                for fc in range(NFC):
                    pg = psG[fc % 2]
                    for dc in range(8):
                        ph.op("pe", MM(pg[:, 0:TLF], wup[:, dc, fc * 128:(fc + 1) * 128], hT[:, dc, 0:TLF], dc == 0, dc == 7),
                              reads=[wup, hT], writes=[pg] if dc == 0 else (), parts=() if dc == 0 else [pg])
                    for dc in range(8):
                        ph.op("pe", MM(psH[:, 0:2], wup[:, dc, fc * 128:(fc + 1) * 128], hT[:, dc, TLF:TLF + 2], dc == 0, dc == 7),
                              reads=[wup, hT], writes=[psH] if dc == 0 else (), parts=() if dc == 0 else [psH])
                    for dc in range(8):
                        ph.op("pe", MM(psV[:, 0:TLF], wup[:, dc, DFF + fc * 128:DFF + (fc + 1) * 128], hT[:, dc, 0:TLF],
                                       dc == 0, dc == 7),
                              reads=[wup, hT], writes=[psV] if dc == 0 else (), parts=() if dc == 0 else [psV])
                    ph.op("act", ACT(gc[:, 0:TLF], pg[:, 0:TLF], AF.Identity, scale=cwc(1, fc), bias=cwc(3, fc)),
                          reads=[pg, cw], writes=[gc])
                    ph.op("dve", STT(gc[:, 1:TLF], pg[:, 0:TLF - 1], cwc(0, fc), gc[:, 1:TLF], ALU.mult, ALU.add),
                          reads=[pg, cw, gc], parts=[gc])
                    ph.op("dve", STT(gc[:, 0:TLF - 1], pg[:, 1:TLF], cwc(2, fc), gc[:, 0:TLF - 1], ALU.mult, ALU.add),
                          reads=[pg, cw, gc], parts=[gc])
                    ph.op("dve", STT(gc[:, 0:1], psH[:, 0:1], cwc(0, fc), gc[:, 0:1], ALU.mult, ALU.add),
                          reads=[psH, cw, gc], parts=[gc])
                    ph.op("dve", STT(gc[:, TLF - 1:TLF], psH[:, 1:2], cwc(2, fc), gc[:, TLF - 1:TLF], ALU.mult, ALU.add),
                          reads=[psH, cw, gc], parts=[gc])
                    ph.op("act", ACT(sg[:, 0:TLF], gc[:, 0:TLF], AF.Silu), reads=[gc], writes=[sg])
                    ph.op("dve", TT(uT[:, fc, :], sg[:, 0:TLF], psV[:, 0:TLF], ALU.mult), reads=[sg, psV], parts=[uT])
                for b in range(nbF):
                    for half in range(2):
                        pd = psD[kd % 2]
                        kd += 1
                        for fc in range(NFC):
                            ph.op("pe", MM(pd[:, :], uT[:, fc, b * 128:(b + 1) * 128], wdn[:, fc, half * 512:(half + 1) * 512],
                                           fc == 0, fc == NFC - 1),
                                  reads=[uT, wdn], writes=[pd] if fc == 0 else (), parts=() if fc == 0 else [pd])
                        ph.op("dve", TT(xt[:, b, half * 512:(half + 1) * 512], pd[:, :], xt[:, b, half * 512:(half + 1) * 512],
                                        ALU.add), reads=[pd, xt], parts=[xt])
                tok0 = s * CH + t0
                if not final:
                    ph.dma("pool", "st_xt", self.x1[tok0:tok0 + TLF, :].rearrange("(b p) d -> p b d", p=128),
                           xt[:, 0:nbF, :], reads=[xt], parts=[self.x1])
                else:
                    junk = B["junk"]
                    for b in range(nbF):
                        ph.op("act", ACT(junk[:, :], xt[:, b, :], AF.Square, accum_out=ss2[:, b:b + 1]),
                              reads=[xt], writes=[junk], parts=[ss2])
                    ph.op("act", ACT(rt2[:, 0:nbF], ss2[:, 0:nbF], AF.Sqrt, scale=1.0 / D, bias=cols[:, 47:48]),
                          reads=[ss2, cols], writes=[rt2])
                    ph.op("dve", RECIP(rs2[:, 0:nbF], rt2[:, 0:nbF]), reads=[rt2], writes=[rs2])
                    for b in range(nbF):
                        ph.op("dve", STT(xt[:, b, :], xt[:, b, :], rs2[:, b:b + 1], gf[:, :], ALU.mult, ALU.mult),
                              reads=[xt, rs2, gf], parts=[xt])
                    ph.dma("pool", "st_xt", self.y[tok0:tok0 + TLF, :].rearrange("(b p) d -> p b d", p=128),
                           xt[:, 0:nbF, :], reads=[xt], parts=[self.y])
        ph.finish()

    def phase_P(self):
        ph = Phase(self.cx, "P")
        CH, TL, NT = self.CH, self.TL, self.NT
        C = self.load_consts(ph, ["ident", "rt", "ones"])
        cols = C["cols"]
        wd = ph.sb("wd", [128, 8, 768], BF16)
        wuq = ph.sb("wuq", [128, 3, 1536], BF16)
        wukv = ph.sb("wukv", [128, 2, 2048], BF16)
        stg = [ph.sb(f"stg{i}", [128, 2048], F32) for i in range(2)]
        self.load_w_scaled(ph, wd, self.w_dn1, 8, 768, cols, 16, stg)
        self.load_w_scaled(ph, wuq, self.w_uq, 3, 1536, cols, 34, stg)
        self.load_w_scaled(ph, wukv, self.w_ukv, 2, 2048, cols, 37, stg)
        ropes = [ph.sb(f"rope{i}", [128, 2, TL], F32) for i in range(2)]
        ropeM3 = self.ropeM.h.rearrange("p (a t) -> p a t", a=2)
        ropebox = [None]
        B = self.norm_bufs(ph, 4, TL, cols)
        B["src_t"] = self.x1
        hT = B["hT"]
        psP = [ph.ps(f"psP{i}", [128, 512]) for i in range(4)]
        psX = [ph.ps(f"psX{i}", [128, 512]) for i in range(2)]
        sq = ph.sb("sq", [128, 3, 512], BF16)
        rtq = ph.sb("rtq", [128, 512], F32)
        rinv = ph.sb("rinv", [128, 512], F32)
        cqn = ph.sb("cqn", [128, 3, 512], BF16)
        ckvn = ph.sb("ckvn", [128, 2, 512], BF16)
        qr = ph.sb("qr", [128, 512], BF16)
        t1 = ph.sb("t1", [128, 512], F32)
        t2 = ph.sb("t2", [128, 512], F32)
        ob = [ph.sb(f"ob{i}", [128, 512], BF16) for i in range(3)]
        vo = [ph.sb(f"vo{i}", [128, 1024], BF16) for i in range(2)]
        kp = [0]
        ko = [0]
        kv = [0]

        def proj(M, n, lhs_fn, rhs_fn, nk):
            pp = psP[kp[0] % 4]
            kp[0] += 1
            for c in range(nk):
                lt, la = lhs_fn(c)
                rt_, ra = rhs_fn(c)
                ph.op("pe", MM(pp[0:M, 0:n], la, ra, c == 0, c == nk - 1), reads=[lt, rt_],
                      writes=[pp] if c == 0 else (), parts=() if c == 0 else [pp])
            return pp

        def lownorm(nch, wc0, dim, dst):
            pps = []
            for c in range(nch):
                pp = proj(128, TL, lambda dc, c=c: (wd, wd[:, dc, wc0 + c * 128:wc0 + (c + 1) * 128]),
                          lambda dc: (hT, hT[:, dc, 0:TL]), 8)
                ph.op("act", ACT(sq[:, c, :], pp[:, :], AF.Square), reads=[pp], parts=[sq])
                pps.append(pp)
            px = psX[0]
            for c in range(nch):
                ph.op("pe", MM(px[:, :], C["ones"][:], sq[:, c, :], c == 0, c == nch - 1), reads=[C["ones"], sq],
                      writes=[px] if c == 0 else (), parts=() if c == 0 else [px])
            ph.op("act", ACT(rtq[:, :], px[:, :], AF.Sqrt, scale=1.0 / dim, bias=cols[:, 47:48]),
                  reads=[px, cols], writes=[rtq])
            ph.op("dve", RECIP(rinv[:, :], rtq[:, :]), reads=[rtq], writes=[rinv])
            for c in range(nch):
                ph.op("dve", TT(dst[:, c, :], pps[c][:, :], rinv[:, :], ALU.mult), reads=[pps[c], rinv], parts=[dst])

        def rope_apply(src, M, c0):
            o = ob[ko[0] % 3]
            ko[0] += 1
            px2 = psX[1]
            ph.op("pe", MM(px2[0:M, :], C["rt"][0:M, 0:M], src[0:M, :]), reads=[C["rt"], src], writes=[px2])
            rope = ropebox[0]
            ph.op("dve", TT(t1[0:M, :], src[0:M, :], rope[0:M, 0, :], ALU.mult), reads=[src, rope], writes=[t1])
            ph.op("dve", TT(t2[0:M, :], px2[0:M, :], rope[0:M, 1, :], ALU.mult),
                  reads=[px2, rope], writes=[t2])
            ph.op("dve", TT(o[0:M, :], t1[0:M, :], t2[0:M, :], ALU.add), reads=[t1, t2], writes=[o])
            return o

        def plain(pp, M):
            o = ob[ko[0] % 3]
            ko[0] += 1
            ph.op("act", ACT(o[0:M, :], pp[0:M, :], AF.Copy), reads=[pp], writes=[o])
            return o

        for s in range(NSEQ):
            for j in range(NT):
                tok0 = s * CH + j * TL
                c0 = j * TL
                self.norm_T(ph, B, self.x1[tok0:tok0 + TL, :], TL // 128, ident=C["ident"])
                ropebox[0] = ropes[j % 2]
                ph.dma("sp", "ld_" + ropebox[0].name, ropebox[0][:, :, :], ropeM3[:, :, j * TL:(j + 1) * TL],
                       reads=[self.ropeM], writes=[ropebox[0]])
                lownorm(3, 0, 384, cqn)
                lownorm(2, 384, 256, ckvn)
                pp = proj(128, TL, lambda dc: (wd, wd[:, dc, 640:768]), lambda dc: (hT, hT[:, dc, 0:TL]), 8)
                ph.op("act", ACT(qr[:, :], pp[:, :], AF.Copy), reads=[pp], writes=[qr])
                o = rope_apply(qr, 128, c0)
                ph.dma("pool", "st_" + o.name, self.krT_loc[s * 32:(s + 1) * 32, c0:c0 + TL], o[96:128, :],
                       reads=[o], parts=[self.krT_loc])
                for i in range(8):
                    pp = proj(128, TL, lambda c, i=i: (wuq, wuq[:, c, i * 128:(i + 1) * 128]),
                              lambda c: (cqn, cqn[:, c, :]), 3)
                    o = plain(pp, 128)
                    for hh in range(2):
                        h = 2 * i + hh
                        ph.dma("pool", "st_" + o.name, self.qT[h * 96:h * 96 + 64, tok0:tok0 + TL], o[hh * 64:(hh + 1) * 64, :],
                               reads=[o], parts=[self.qT])
                for i in range(4):
                    pp = proj(128, TL, lambda c, i=i: (wuq, wuq[:, c, 1024 + i * 128:1024 + (i + 1) * 128]),
                              lambda c: (cqn, cqn[:, c, :]), 3)
                    ph.op("act", ACT(qr[:, :], pp[:, :], AF.Copy), reads=[pp], writes=[qr])
                    o = rope_apply(qr, 128, c0)
                    for hh in range(4):
                        h = 4 * i + hh
                        ph.dma("pool", "st_" + o.name, self.qT[h * 96 + 64:h * 96 + 96, tok0:tok0 + TL],
                               o[hh * 32:(hh + 1) * 32, :], reads=[o], parts=[self.qT])
                for i in range(8):
                    pp = proj(128, TL, lambda c, i=i: (wukv, wukv[:, c, i * 128:(i + 1) * 128]),
                              lambda c: (ckvn, ckvn[:, c, :]), 2)
                    o = plain(pp, 128)
                    ph.dma("pool", "st_" + o.name, self.knT_loc[s * 1024 + i * 128:s * 1024 + (i + 1) * 128, c0:c0 + TL],
                           o[:, :], reads=[o], parts=[self.knT_loc])
                for b in range(TL // 128):
                    v = vo[kv[0] % 2]
                    kv[0] += 1
                    for half in range(2):
                        pp = proj(128, 512, lambda c, b=b: (ckvn, ckvn[:, c, b * 128:(b + 1) * 128]),
                                  lambda c, half=half: (wukv, wukv[:, c, 1024 + half * 512:1024 + (half + 1) * 512]), 2)
                        ph.op("act", ACT(v[:, half * 512:(half + 1) * 512], pp[:, :], AF.Copy), reads=[pp],
                              writes=[v] if half == 0 else (), parts=() if half == 0 else [v])
                    ph.dma("pool", "st_" + v.name, self.v_loc[tok0 + b * 128:tok0 + (b + 1) * 128, :], v[:, :],
                           reads=[v], parts=[self.v_loc])
        ph.finish()

    def phase_H(self):
        ph = Phase(self.cx, "H")
        CH, TL, NT, NB, CB, NTOK, S = self.CH, self.TL, self.NT, self.NB, self.CB, self.NTOK, self.S
        Ks = [ph.sb(f"Ks{i}", [128, S], BF16) for i in range(2)]
        Vs = [ph.sb(f"Vs{i}", [128, NB, 192], BF16) for i in range(2)]
        A = self.attn_bufs(ph, ["m"])
        for i in range(2):
            ph.op("dve", MEMSET(Vs[i][:, :, 64:128], 1.0), writes=[Vs[i]])
            ph.op("dve", MEMSET(Ks[i][64:128, :], 0.0), writes=[Ks[i]])
        heads = [(s, h) for s in range(NSEQ) for h in range(16)]

        def load_head(idx):
            if idx >= len(heads):
                return
            s, h = heads[idx]
            K = Ks[idx % 2]
            for r in range(NCX):
                base = (r * NSEQ + s) * 1024 + h * 64
                ph.dma("sp", "ld_" + K.name, K[0:64, r * CH:(r + 1) * CH], self.knT_all[base:base + 64, :],
                       reads=[self.knT_all], parts=[K])
                rb = (r * NSEQ + s) * 32
                ph.dma("sp", "ld_" + K.name, K[64:96, r * CH:(r + 1) * CH], self.krT_all[rb:rb + 32, :],
                       reads=[self.krT_all], parts=[K])
            if h % 2 == 0:
                V = Vs[(idx // 2) % 2]
                for r in range(NCX):
                    src = self.v_all[r * NTOK + s * CH:r * NTOK + (s + 1) * CH, :].rearrange("(b p) c -> p b c", p=128)
                    for hh in range(2):
                        ph.dma("sp", "ld_" + V.name, V[:, r * CB:(r + 1) * CB, hh * 128:hh * 128 + 64],
                               src[:, :, (h + hh) * 64:(h + hh + 1) * 64], reads=[self.v_all], parts=[V])

        load_head(0)
        scale = 96.0 ** -0.5
        for idx, (s, h) in enumerate(heads):
            K = Ks[idx % 2]
            V = Vs[(idx // 2) % 2]
            odd = h % 2
            for j in range(NT):
                tok0 = s * CH + j * TL
                self.attn_dense_unit(
                    ph, A, "m", self.qT, self.qT[h * 96:(h + 1) * 96, tok0:tok0 + TL], slice(0, 96),
                    K, lambda i, K=K: K[:, i * 128:(i + 1) * 128],
                    V, lambda i, V=V, odd=odd: V[:, i, odd * 64:odd * 64 + 128],
                    NB, scale, odd == 0, self.oT, self.oT[h * 64:(h + 1) * 64, tok0:tok0 + TL],
                    after_qload=(lambda idx=idx: load_head(idx + 1)) if j == 0 else None)
        ph.finish()

    def build(self):
        nc = bass.Bass("TRN2", target_bir_lowering=False)
        self.nc = nc
        self.cx = Ctx(nc)
        self.declare()
        with self.cx.stack:
            stages = self.stages if hasattr(self, "stages") else "ABCOXFPGHQYZ"
            if "A" in stages:
                self.phase_A()
            if "B" in stages and NCX > 1:
                self.phase_AG("G0", [(self.kaT_loc, self.kaT_all), (self.va_loc, self.va_all)])
            if "a" in stages:
                self.phase_A("A2")
            if "C" in stages:
                self.phase_C()
            if "O" in stages:
                self.phase_O("O0", self.w_out0, self.x_own)
            if "X" in stages and NCX > 1:
                self.phase_X("X0")
            if "F" in stages:
                self.phase_F("F0", 0, False)
            if "P" in stages:
                self.phase_P()
            if "G" in stages and NCX > 1:
                self.phase_AG("G1", [(self.knT_loc, self.knT_all), (self.krT_loc, self.krT_all),
                                     (self.v_loc, self.v_all)])
            if "H" in stages:
                self.phase_H()
            if "Q" in stages:
                self.phase_O("O1", self.w_out1, self.x1)
            if "Y" in stages and NCX > 1:
                self.phase_X("X1")
            if "Z" in stages:
                self.phase_F("F1", 1, True)
            if self.debug:
                self.debug_dump()
        return nc

    def debug_dump(self):
        ph = Phase(self.cx, "DBG")
        for name in self.debug:
            t = self.cx.dram[name]
            shape = list(t.h.shape)
            o = self.cx.dram_t("dbg_" + name, shape, t.h.dtype, kind="ExternalOutput")
            ph.dma("sp", "dbg_" + name, o.h, t.h, reads=[t], writes=[o])
        ph.finish()


def _rope_tables(CH, core):
    theta = 10000.0
    inv = (theta ** (-np.arange(0, 32, 2, dtype=np.float32) / np.float32(32))).astype(np.float32)
    t = np.arange(core * CH, (core + 1) * CH)
    row = (t // 64).astype(np.float32)
    col = (t % 64).astype(np.float32)
    pos = t.astype(np.float32)
    ang_r = row[:, None] * inv[None, :]
    ang_c = col[:, None] * inv[None, :]
    ang_t = pos[:, None] * inv[None, :]
    A = np.zeros((128, 2 * CH), np.float32)
    M = np.zeros((128, 2 * CH), np.float32)
    for p in range(128):
        d = p % 64
        a = ang_r[:, d % 16] if d < 32 else ang_c[:, (d - 32) % 16]
        A[p, :CH] = np.cos(a)
        A[p, CH:] = np.sin(a)
        m = ang_t[:, (p % 32) % 16]
        M[p, :CH] = np.cos(m)
        M[p, CH:] = np.sin(m)
    return A, M


def _consts():
    bf = ml_dtypes.bfloat16
    ident = np.eye(128, dtype=np.float32).astype(bf)
    rt = np.zeros((128, 128), np.float32)
    for blk in range(4):
        for i in range(16):
            rt[blk * 32 + i + 16, blk * 32 + i] = -1.0
            rt[blk * 32 + i, blk * 32 + i + 16] = 1.0
    bd = np.zeros((128, 128), np.float32)
    bd[:64, :64] = 1.0
    bd[64:, 64:] = 1.0
    ones = np.ones((128, 128), np.float32)
    slopes = (2.0 ** (-8.0 * np.arange(1, 9, dtype=np.float32) / 8)).astype(np.float32)
    wb = np.zeros((128, 8, 3, 128), np.float32)
    sl = np.arange(128)
    for r in range(3):
        s_rel = (2 - r - 1) * 128 + sl
        dist = np.abs(np.arange(128)[None, :] - s_rel[:, None])
        for h in range(8):
            wb[:, h, r, :] = np.where(dist <= 128, -slopes[h] * dist.astype(np.float32), NEG)
    return ident, rt.astype(bf), bd.astype(bf), ones.astype(bf), wb.reshape(128, -1)


def _colarr(v, n):
    return np.ascontiguousarray(np.asarray(v, np.float32).reshape(n, 128).T)


_CACHE = {}


def kernel(x_prompt, x_sample, norm_mix, norm_ffn, norm_final, e_w_in, e_q_gain, e_k_gain, e_sink, e_w_out,
           o_w_down, o_q_gain, o_kv_gain, o_w_uq, o_w_ukv, o_w_out, f_w_up, f_conv_w, f_conv_b, f_w_down,
           _stages=None, _debug=None):
    f32 = np.float32
    xs = np.concatenate([np.asarray(x_prompt, f32), np.asarray(x_sample, f32)], axis=0)
    NS_ALL = xs.shape[0]
    S = xs.shape[1]
    CH = S
    key = (S, _stages, tuple(_debug) if _debug else None)
    if key not in _CACHE:
        b = Builder(S, debug=_debug)
        if _stages is not None:
            b.stages = _stages
        _CACHE[key] = b.build()
    nc = _CACHE[key]

    ident, rt, bd, ones, wb = _consts()
    w_in = np.asarray(e_w_in[0], f32)
    w_in_p = np.ascontiguousarray(np.concatenate(
        [w_in[:, 0:512], w_in[:, 512:640], w_in[:, 768:1280], w_in[:, 1280:1408], w_in[:, 640:768], w_in[:, 1408:1536]], 1))
    uq = np.asarray(o_w_uq[0], f32).reshape(384, 16, 96)
    w_uq_p = np.ascontiguousarray(np.concatenate([uq[:, :, :64].reshape(384, 1024), uq[:, :, 64:].reshape(384, 512)], 1))
    ukv = np.asarray(o_w_ukv[0], f32).reshape(256, 16, 128)
    w_ukv_p = np.ascontiguousarray(np.concatenate([ukv[:, :, :64].reshape(256, 1024), ukv[:, :, 64:].reshape(256, 1024)], 1))
    convw = np.zeros((128, 2, 4, NFC), f32)
    for l in range(2):
        for q in range(3):
            convw[:, l, q, :] = _colarr(f_conv_w[l][q], NFC)
        convw[:, l, 3, :] = _colarr(f_conv_b[l], NFC)
    convw = convw.reshape(128, -1)
    shared = {
        "w_in": w_in_p, "w_out0": np.ascontiguousarray(e_w_out[0], f32), "w_dn1": np.ascontiguousarray(np.concatenate(
            [np.asarray(o_w_down[0], f32)[:, :640], np.zeros((D, 96), f32), np.asarray(o_w_down[0], f32)[:, 640:]], 1)),
        "w_uq": w_uq_p, "w_ukv": w_ukv_p, "w_out1": np.ascontiguousarray(o_w_out[0], f32),
        "w_up0": np.ascontiguousarray(f_w_up[0], f32), "w_up1": np.ascontiguousarray(f_w_up[1], f32),
        "w_down0": np.ascontiguousarray(f_w_down[0], f32), "w_down1": np.ascontiguousarray(f_w_down[1], f32),
        "convw": convw, "gfin": np.ascontiguousarray(norm_final, f32),
        "c_ident": ident, "c_rt": rt, "c_bd": bd, "c_ones": ones, "wbias": wb,
    }
    cols = np.zeros((128, 64), f32)
    cols[:, 0:8] = _colarr(norm_mix[0], 8)
    cols[:, 8:16] = _colarr(norm_ffn[0], 8)
    cols[:, 16:24] = _colarr(norm_mix[1], 8)
    cols[:, 24:32] = _colarr(norm_ffn[1], 8)
    cols[:, 32] = np.tile(np.asarray(e_q_gain[0], f32), 2)
    cols[:, 33] = np.tile(np.asarray(e_k_gain[0], f32), 2)
    cols[:, 34:37] = _colarr(o_q_gain[0], 3)
    cols[:, 37:39] = _colarr(o_kv_gain[0], 2)
    cols[:, 39:47] = np.asarray(e_sink[0], f32)[None, :]
    cols[:, 47] = EPS
    cols[:, 48] = NEG
    cols[:, 49] = NEG
    ra, rm = _rope_tables(CH, 0)
    shared.update({"cols": cols, "ropeA": ra, "ropeM": rm, "x_halo": np.zeros((NSEQ * 256, D), f32)})
    in_maps = []
    for c in range(NCORES):
        m = dict(shared)
        m["x_own"] = np.ascontiguousarray(xs[c]) if c < NS_ALL else np.zeros((S, D), f32)
        in_maps.append(m)
    res = run_bass_kernel_spmd(nc, in_maps, core_ids=list(range(NCORES)))
    if _debug:
        return res.results
    y = np.stack([np.asarray(res.results[c]["y"], f32).reshape(S, D) for c in range(NS_ALL)], 0)
    nb = np.asarray(x_prompt).shape[0]
    return (y[:nb], y[nb:])
```
